# Optimizing a Trainium2 kernel written in Bass

```python
import jax, jax.numpy as jnp
from jax import lax
import numpy as np

D_MODEL = 1024
BATCH = 16
SEQ = 2048
DEPTH = 4

CTX_LEN = 256
GRID_W = 64
HEAD_DIM = 64
A_HEADS = 8
A_KV_HEADS = 2
A_WINDOW = 128
A_BLOCK = 128
B_HEADS = 8
NA_KH = 8
NA_KW = 16
NA_QCOLS = 16
NA_BAND = 32
C_HEADS = 4
C_DK = 128
C_DV = 128
D_HEADS = 4
D_DK = 64
D_DV = 128
D_GATE_RANK = 16
D_GATE_NORM = 16.0
CHUNK = 32
N_BRANCH = 4
D_FF = 4 * D_MODEL
ROPE_BASE = 10000.0
EPS = 1e-6
NEG_INF = -1e30

A_QW = A_HEADS * HEAD_DIM
A_KVW = A_KV_HEADS * HEAD_DIM
B_W = B_HEADS * HEAD_DIM
C_KW = C_HEADS * C_DK
C_VW = C_HEADS * C_DV
D_KW = D_HEADS * D_DK
D_VW = D_HEADS * D_DV
BRANCH_W = 512
IN_NAMES = ('a_q', 'a_k', 'a_v', 'b_q', 'b_k', 'b_v', 'c_q', 'c_f_fwd', 'c_f_bwd', 'c_i', 'c_g',
            'd_q', 'd_k', 'd_v', 'd_gk_fwd', 'd_gk_bwd', 'd_g', 'merge')
IN_SIZES = (A_QW, A_KVW, A_KVW, B_W, B_W, B_W, C_KW, C_KW, C_KW, C_VW, C_VW,
            D_KW, D_KW, D_VW, D_GATE_RANK, D_GATE_RANK, D_VW, N_BRANCH * D_MODEL)
IN_WIDTH = sum(IN_SIZES)

kernel_name = 'hybrid_bidir_diffusion_trunk'


def rms_norm(x, g):
    xf = x.astype(jnp.float32)
    y = xf * lax.rsqrt(jnp.mean(jnp.square(xf), axis=-1, keepdims=True) + EPS)
    return (y * g.astype(jnp.float32)).astype(x.dtype)


def heads(t, nh):
    return t.reshape(t.shape[:-1] + (nh, t.shape[-1] // nh))


def flip(t):
    return t[:, ::-1]


def split_in(z):
    out = {}
    off = 0
    for name, size in zip(IN_NAMES, IN_SIZES):
        out[name] = z[..., off:off + size]
        off += size
    return out


def axial_angles(n):
    t = jnp.arange(n)
    row = (t // GRID_W).astype(jnp.float32)
    col = (t % GRID_W).astype(jnp.float32)
    half = HEAD_DIM // 2
    inv = ROPE_BASE ** (-jnp.arange(0, half, 2, dtype=jnp.float32) / half)
    return row[:, None] * inv, col[:, None] * inv


def rope_1d(x, ang):
    f = x.shape[-1] // 2
    cos = jnp.cos(ang)[:, None, :].astype(x.dtype)
    sin = jnp.sin(ang)[:, None, :].astype(x.dtype)
    x1, x2 = x[..., :f], x[..., f:]
    return jnp.concatenate([x1 * cos - x2 * sin, x2 * cos + x1 * sin], axis=-1)


def axial_rope(x, ang_r, ang_c):
    half = HEAD_DIM // 2
    return jnp.concatenate([rope_1d(x[..., :half], ang_r), rope_1d(x[..., half:], ang_c)], axis=-1)


def ctx_attention(q, k, v, sink):
    bsz, L, nh, hd = q.shape
    nkv = k.shape[2]
    grp = nh // nkv
    qg = q.reshape(bsz, L, nkv, grp, hd)
    s = jnp.einsum('bqkgd,bskd->bkgqs', qg, k).astype(jnp.float32) * (hd ** -0.5)
    if sink is not None:
        s_sink = jnp.broadcast_to(sink.reshape(nkv, grp)[None, :, :, None, None].astype(jnp.float32), (bsz, nkv, grp, L, 1))
        s = jnp.concatenate([s, s_sink], axis=-1)
    p = jax.nn.softmax(s, axis=-1)[..., :L].astype(v.dtype)
    o = jnp.einsum('bkgqs,bskd->bqkgd', p, v)
    return o.reshape(bsz, L, nh * hd)


def window_attention(q, k, v, kc, vc, sink):
    bsz, n, nh, hd = q.shape
    nkv = k.shape[2]
    grp = nh // nkv
    nb = n // A_BLOCK
    L = kc.shape[1]
    scale = hd ** -0.5
    qb = q.reshape(bsz, nb, A_BLOCK, nkv, grp, hd).swapaxes(0, 1)

    def bands(t):
        tp = jnp.pad(t, ((0, 0), (A_BLOCK, A_BLOCK), (0, 0), (0, 0))).reshape(bsz, nb + 2, A_BLOCK, nkv, hd)
        return jnp.concatenate([tp[:, :-2], tp[:, 1:-1], tp[:, 2:]], axis=2).swapaxes(0, 1)

    kb, vb = bands(k), bands(v)
    blk = jnp.arange(nb)[:, None, None]
    qpos = blk * A_BLOCK + jnp.arange(A_BLOCK)[None, :, None]
    kpos = (blk - 1) * A_BLOCK + jnp.arange(3 * A_BLOCK)[None, None, :]
    valid = (jnp.abs(kpos - qpos) <= A_WINDOW) & (kpos >= 0) & (kpos < n)
    sink_s = jnp.broadcast_to(sink.reshape(nkv, grp)[None, :, :, None, None].astype(jnp.float32),
                              (bsz, nkv, grp, A_BLOCK, 1))
    wlen = 3 * A_BLOCK

    def block(args):
        qi, ki, vi, mi = args
        s_win = jnp.einsum('bqkgd,bskd->bkgqs', qi, ki).astype(jnp.float32) * scale
        s_win = jnp.where(mi, s_win, NEG_INF)
        s_ctx = jnp.einsum('bqkgd,bckd->bkgqc', qi, kc).astype(jnp.float32) * scale
        p = jax.nn.softmax(jnp.concatenate([s_win, s_ctx, sink_s], axis=-1), axis=-1).astype(v.dtype)
        o = (jnp.einsum('bkgqs,bskd->bqkgd', p[..., :wlen], vi)
             + jnp.einsum('bkgqc,bckd->bqkgd', p[..., wlen:wlen + L], vc))
        return o.reshape(bsz, A_BLOCK, nh * hd)

    out = lax.map(block, (qb, kb, vb, valid))
    return out.swapaxes(0, 1).reshape(bsz, n, nh * hd)


def neighbourhood_attention(q, k, v, kc, vc, rel_bias):
    bsz, n, nh, hd = q.shape
    rows = n // GRID_W
    kh = min(NA_KH, rows)
    ncb = GRID_W // NA_QCOLS
    band_starts = [min(max(j * NA_QCOLS - NA_KW // 2, 0), GRID_W - NA_BAND) for j in range(ncb)]
    scale = hd ** -0.5
    qr_all = q.reshape(bsz, rows, ncb, NA_QCOLS, nh, hd).swapaxes(0, 1)
    kg = k.reshape(bsz, rows, GRID_W, nh, hd)
    vg = v.reshape(bsz, rows, GRID_W, nh, hd)
    qcol = (jnp.arange(ncb)[:, None] * NA_QCOLS + jnp.arange(NA_QCOLS)[None, :])[:, :, None]
    kcol = jnp.asarray(band_starts, dtype=jnp.int32)[:, None, None] + jnp.arange(NA_BAND)[None, None, :]
    cstart = jnp.clip(qcol - NA_KW // 2, 0, GRID_W - NA_KW)
    col_ok = (kcol >= cstart) & (kcol < cstart + NA_KW)
    dc_idx = jnp.clip(kcol - qcol, -(NA_KW - 1), NA_KW - 1) + (NA_KW - 1)
    wlen = kh * NA_BAND

    def row_block(args):
        r, qr = args
        r0 = jnp.clip(r - kh // 2, 0, rows - kh)
        kr = lax.dynamic_slice_in_dim(kg, r0, kh, axis=1)
        vr = lax.dynamic_slice_in_dim(vg, r0, kh, axis=1)
        kb = jnp.stack([kr[:, :, s:s + NA_BAND] for s in band_starts], axis=1)
        vb = jnp.stack([vr[:, :, s:s + NA_BAND] for s in band_starts], axis=1)
        dr_idx = r0 + jnp.arange(kh) - r + (NA_KH - 1)
        bias = rel_bias[:, dr_idx[None, None, :, None], dc_idx[:, :, None, :]]
        s_loc = jnp.einsum('bjqhd,bjrchd->bhjqrc', qr, kb).astype(jnp.float32) * scale
        s_loc = jnp.where(col_ok[:, :, None, :], s_loc + bias[None].astype(jnp.float32), NEG_INF)
        s_loc = s_loc.reshape(bsz, nh, ncb, NA_QCOLS, wlen)
        s_ctx = jnp.einsum('bjqhd,bchd->bhjqc', qr, kc).astype(jnp.float32) * scale
        p = jax.nn.softmax(jnp.concatenate([s_loc, s_ctx], axis=-1), axis=-1).astype(v.dtype)
        o = (jnp.einsum('bhjqs,bjshd->bjqhd', p[..., :wlen], vb.reshape(bsz, ncb, wlen, nh, hd))
             + jnp.einsum('bhjqc,bchd->bjqhd', p[..., wlen:], vc))
        return o.reshape(bsz, GRID_W, nh * hd)

    out = lax.map(row_block, (jnp.arange(rows), qr_all))
    return out.swapaxes(0, 1).reshape(bsz, n, nh * hd)


def chunk_scan(q, k, v, g, s0):
    bsz, n, nh, dk = k.shape
    dv = v.shape[-1]
    nc = n // CHUNK
    f32 = jnp.float32
    if s0 is None:
        s0 = jnp.zeros((bsz, nh, dk, dv), f32)

    def chunks(t):
        return t.astype(f32).reshape(bsz, nc, CHUNK, nh, t.shape[-1]).transpose(1, 0, 3, 2, 4)

    tri = jnp.tril(jnp.ones((CHUNK, CHUNK), dtype=bool))[:, :, None]

    def step(state, inp):
        qi, ki, vi, gi = inp
        cum = jnp.cumsum(gi, axis=2)
        rel = jnp.exp(jnp.where(tri, cum[:, :, :, None, :] - cum[:, :, None, :, :], NEG_INF))
        attn = jnp.einsum('bhtk,bhsk,bhtsk->bhts', qi, ki, rel)
        o = jnp.einsum('bhts,bhsv->bhtv', attn, vi) + jnp.einsum('bhtk,bhkv->bhtv', qi * jnp.exp(cum), state)
        last = cum[:, :, -1:, :]
        new_state = (jnp.exp(last[:, :, 0, :, None]) * state
                     + jnp.einsum('bhsk,bhsv->bhkv', ki * jnp.exp(last - cum), vi))
        return new_state, o

    s_fin, o = lax.scan(step, s0, (chunks(q), chunks(k), chunks(v), chunks(g)))
    return o.transpose(1, 0, 3, 2, 4).reshape(bsz, n, nh, dv).astype(v.dtype), s_fin


def final_state(k, v, g):
    cum = jnp.cumsum(g.astype(jnp.float32), axis=1)
    w = k.astype(jnp.float32) * jnp.exp(cum[:, -1:] - cum)
    return jnp.einsum('bnhk,bnhv->bhkv', w, v.astype(jnp.float32))


def bidir_recurrence(lat_f, lat_b, ctx_f, ctx_b, need_ctx):
    if need_ctx:
        oc_f, s_f = chunk_scan(*ctx_f, None)
        oc_b, s_b = chunk_scan(*[flip(t) for t in ctx_b], None)
        o_ctx = oc_f + flip(oc_b)
    else:
        s_f = final_state(*ctx_f[1:])
        s_b = final_state(*[flip(t) for t in ctx_b[1:]])
        o_ctx = None
    o_f, _ = chunk_scan(*lat_f, s_f)
    o_b, _ = chunk_scan(*[flip(t) for t in lat_b], s_b)
    return o_f + flip(o_b), o_ctx


def gated_group_norm(o, gain, gate_raw, nh):
    y = rms_norm(o, gain) * jax.nn.silu(heads(gate_raw, nh)).astype(o.dtype)
    return y.reshape(y.shape[:-2] + (-1,))


def merge_branches(branches, gate_logits, w_branch, w_out):
    g = heads(gate_logits, N_BRANCH)
    acc = None
    for j, yb in enumerate(branches):
        t = jax.nn.sigmoid(g[..., j, :]) * (yb @ w_branch[j])
        acc = t if acc is None else acc + t
    return acc @ w_out


def channel_mlp(h, w1, w2):
    return jnp.square(jax.nn.relu(h @ w1)) @ w2


def token_mixer(h, hc, w_in, a_sink, b_rel_bias, lb, c_norm, d_gate_up, d_gate_bias, d_norm,
                w_branch, w_out, need_ctx):
    n = h.shape[1]
    f32 = jnp.float32
    z = split_in(h @ w_in)
    zc = split_in(hc @ w_in)

    ang_r, ang_c = axial_angles(n)
    aq = axial_rope(heads(z['a_q'], A_HEADS), ang_r, ang_c)
    ak = axial_rope(heads(z['a_k'], A_KV_HEADS), ang_r, ang_c)
    akc, avc = heads(zc['a_k'], A_KV_HEADS), heads(zc['a_v'], A_KV_HEADS)
    y_a = window_attention(aq, ak, heads(z['a_v'], A_KV_HEADS), akc, avc, a_sink)

    bkc, bvc = heads(zc['b_k'], B_HEADS), heads(zc['b_v'], B_HEADS)
    y_b = neighbourhood_attention(heads(z['b_q'], B_HEADS), heads(z['b_k'], B_HEADS),
                                  heads(z['b_v'], B_HEADS), bkc, bvc, b_rel_bias)

    lbh = lb.reshape(C_HEADS, C_DK)

    def hgrn_forget(zf):
        zf = heads(zf, C_HEADS).astype(f32)
        log_f = jnp.log(lbh + (1.0 - lbh) * jax.nn.sigmoid(zf))
        key = (1.0 - lbh) * jax.nn.sigmoid(-zf)
        return key, log_f

    def hgrn_inputs(zz):
        q = jax.nn.silu(heads(zz['c_q'], C_HEADS))
        v = heads(zz['c_i'], C_HEADS)
        kf, gf = hgrn_forget(zz['c_f_fwd'])
        kb, gb = hgrn_forget(zz['c_f_bwd'])
        return (q, kf, v, gf), (q, kb, v, gb)

    lat_cf, lat_cb = hgrn_inputs(z)
    ctx_cf, ctx_cb = hgrn_inputs(zc)
    o_c, oc_c = bidir_recurrence(lat_cf, lat_cb, ctx_cf, ctx_cb, need_ctx)
    y_c = gated_group_norm(o_c, c_norm, z['c_g'], C_HEADS)

    def gla_inputs(zz):
        q = heads(zz['d_q'], D_HEADS) * (D_DK ** -0.5)
        k = heads(zz['d_k'], D_HEADS)
        v = heads(zz['d_v'], D_HEADS)

        def gate(zd, j):
            return heads(jax.nn.log_sigmoid((zd @ d_gate_up[j] + d_gate_bias[j]).astype(f32)) / D_GATE_NORM, D_HEADS)

        return (q, k, v, gate(zz['d_gk_fwd'], 0)), (q, k, v, gate(zz['d_gk_bwd'], 1))

    lat_df, lat_db = gla_inputs(z)
    ctx_df, ctx_db = gla_inputs(zc)
    o_d, oc_d = bidir_recurrence(lat_df, lat_db, ctx_df, ctx_db, need_ctx)
    y_d = gated_group_norm(o_d, d_norm, z['d_g'], D_HEADS)

    y = merge_branches([y_a, y_b, y_c, y_d], z['merge'], w_branch, w_out)
    if need_ctx:
        yc_a = ctx_attention(heads(zc['a_q'], A_HEADS), akc, avc, a_sink)
        yc_b = ctx_attention(heads(zc['b_q'], B_HEADS), bkc, bvc, None)
        yc_c = gated_group_norm(oc_c, c_norm, zc['c_g'], C_HEADS)
        yc_d = gated_group_norm(oc_d, d_norm, zc['d_g'], D_HEADS)
        y_ctx = merge_branches([yc_a, yc_b, yc_c, yc_d], zc['merge'], w_branch, w_out)
    else:
        y_ctx = None
    return y, y_ctx


def setup_inputs(seed: int = 0) -> dict:
    key = jax.random.key(seed)
    ks = jax.random.split(key, 20)
    f32 = jnp.float32

    def nrm(k, shape, s):
        return jax.random.normal(k, shape, f32) * s

    return {
        'x': nrm(ks[0], (BATCH, SEQ, D_MODEL), 1.0),
        'c': nrm(ks[1], (BATCH, D_MODEL), 1.0),
        'ctx': nrm(ks[2], (BATCH, CTX_LEN, D_MODEL), 1.0),
        'c_ctx': nrm(ks[3], (D_MODEL,), 1.0),
        'w_mod': nrm(ks[4], (DEPTH, D_MODEL, 6 * D_MODEL), 0.5 * D_MODEL ** -0.5),
        'b_mod': nrm(ks[5], (DEPTH, 6 * D_MODEL), 0.02),
        'norm_gains': 1.0 + nrm(ks[6], (DEPTH, 4, D_MODEL), 0.1),
        'w_in': nrm(ks[7], (DEPTH, D_MODEL, IN_WIDTH), D_MODEL ** -0.5),
        'a_sink': nrm(ks[8], (DEPTH, A_HEADS), 0.5),
        'b_rel_bias': nrm(ks[9], (DEPTH, B_HEADS, 2 * NA_KH - 1, 2 * NA_KW - 1), 0.5),
        'c_lower_bounds': nrm(ks[10], (DEPTH, C_KW), 0.1),
        'c_norm': 1.0 + nrm(ks[11], (DEPTH, C_DV), 0.1),
        'd_gate_up': nrm(ks[12], (DEPTH, 2, D_GATE_RANK, D_KW), D_GATE_RANK ** -0.5),
        'd_gate_bias': nrm(ks[13], (DEPTH, 2, D_KW), 0.1),
        'd_norm': 1.0 + nrm(ks[14], (DEPTH, D_DV), 0.1),
        'w_branch': nrm(ks[15], (DEPTH, N_BRANCH, BRANCH_W, D_MODEL), BRANCH_W ** -0.5),
        'w_out': nrm(ks[16], (DEPTH, D_MODEL, D_MODEL), D_MODEL ** -0.5),
        'w_ff1': nrm(ks[17], (DEPTH, D_MODEL, D_FF), D_MODEL ** -0.5),
        'w_ff2': nrm(ks[18], (DEPTH, D_FF, D_MODEL), D_FF ** -0.5),
    }


def reference(x, c, ctx, c_ctx, w_mod, b_mod, norm_gains, w_in, a_sink, b_rel_bias, c_lower_bounds,
              c_norm, d_gate_up, d_gate_bias, d_norm, w_branch, w_out, w_ff1, w_ff2):
    lb_soft = jax.nn.softmax(c_lower_bounds.astype(jnp.float32), axis=0)
    lb_all = jnp.cumsum(lb_soft, axis=0) - lb_soft[0:1]
    xc = ctx
    for l in range(DEPTH):
        need_ctx = l < DEPTH - 1
        ng = norm_gains[l]
        mod = jax.nn.silu(c) @ w_mod[l] + b_mod[l]
        mod_c = jax.nn.silu(c_ctx) @ w_mod[l] + b_mod[l]
        sh1, sc1, g1, sh2, sc2, g2 = jnp.split(mod[:, None, :], 6, axis=-1)
        csh1, csc1, cg1, csh2, csc2, cg2 = jnp.split(mod_c, 6, axis=-1)
        h = rms_norm(x, ng[0]) * (1.0 + sc1) + sh1
        hc = rms_norm(xc, ng[0]) * (1.0 + csc1) + csh1
        y, y_ctx = token_mixer(h, hc, w_in[l], a_sink[l], b_rel_bias[l], lb_all[l], c_norm[l],
                               d_gate_up[l], d_gate_bias[l], d_norm[l], w_branch[l], w_out[l], need_ctx)
        x = x + g1 * rms_norm(y, ng[1])
        h2 = rms_norm(x, ng[2]) * (1.0 + sc2) + sh2
        x = x + g2 * rms_norm(channel_mlp(h2, w_ff1[l], w_ff2[l]), ng[3])
        if need_ctx:
            xc = xc + cg1 * rms_norm(y_ctx, ng[1])
            hc2 = rms_norm(xc, ng[2]) * (1.0 + csc2) + csh2
            xc = xc + cg2 * rms_norm(channel_mlp(hc2, w_ff1[l], w_ff2[l]), ng[3])
    return x
```

```python
import numpy as np
from contextlib import ExitStack
import concourse.bass as bass
import concourse.mybir as mybir
from concourse.bass_utils import run_bass_kernel_spmd

F32 = mybir.dt.float32
BF16 = mybir.dt.bfloat16
AF = mybir.ActivationFunctionType
ALU = mybir.AluOpType
AX = mybir.AxisListType

NCORES = 8
DEPTH = 4
D = 1024
NB = 2
CTX = 256
SEQ = 2048
T = CTX + SEQ
NT = T // 128
DFF = 4096
EPS = 1e-6
NEG = -1.0e4

ENGS = ('pe', 'act', 'dve', 'pool', 'sp')


class Buf:
    __slots__ = ('name', 'lastw', 'readers', 'dkey', 'dcnt', 'dgen')

    def __init__(self, name=''):
        self.name = name
        self.lastw = None
        self.readers = {}
        self.dkey = None
        self.dcnt = 0
        self.dgen = 0


class TT:
    def __init__(self, t, b):
        self.t = t
        self.b = b

    def __getitem__(self, k):
        return self.t[k]


class Phase:
    EPOCH = 12000

    def __init__(self, nc, name):
        self.nc = nc
        self.name = name
        self.ops = {e: [] for e in ENGS}
        self.count = {e: 0 for e in ENGS}
        self.known = {e: {} for e in ENGS}
        self.keys = []
        self.keyset = set()
        self.ndk = 0
        self.stack = ExitStack()
        self.dma_tokens = {}

    def sb(self, name, shape, dtype):
        t = self.stack.enter_context(self.nc.sbuf_tensor(self.name + '_' + name, list(shape), dtype))
        return TT(t, Buf(name))

    def ps(self, name, shape, dtype=F32):
        esz = 4 if dtype == F32 else 2
        nel = 1
        for d in shape[1:]:
            nel *= d
        nbk = (nel * esz + 2047) // 2048
        t = self.stack.enter_context(self.nc.psum_tensor(self.name + '_' + name, [128, nbk * 2048 // esz], dtype))
        flat = t[:, 0:nel]
        if len(shape) == 2:
            view = flat
        else:
            view = flat.rearrange('p (a b) -> p a b', a=shape[1])
        return TT(view, Buf(name))

    def _key(self, k):
        if k not in self.keyset:
            self.keyset.add(k)
            self.keys.append(k)
        return k

    def _deps(self, eng, reads, writes, extra=()):
        deps = {}

        def add(tok):
            if tok is None:
                return
            k, v, e = tok
            if eng == 'pe' and e == 'pe':
                return
            if deps.get(k, 0) < v:
                deps[k] = v
        for b in reads:
            add(b.lastw)
        for b in writes:
            add(b.lastw)
            for k, (v, e) in b.readers.items():
                add((k, v, e))
        for tok in extra:
            add(tok)
        waits = []
        kn = self.known[eng]
        for k, v in deps.items():
            if kn.get(k, 0) >= v:
                continue
            kn[k] = v
            waits.append((k, v))
        return waits

    def _commit(self, tok, reads, writes):
        k, v, e = tok
        for b in writes:
            b.lastw = tok
            b.readers = {}
        for b in reads:
            if b.readers.get(k, (0, e))[0] <= v:
                b.readers[k] = (v, e)

    def op(self, eng, fn, reads=(), writes=()):
        reads = [r.b if isinstance(r, TT) else r for r in reads]
        writes = [w.b if isinstance(w, TT) else w for w in writes]
        waits = self._deps(eng, reads, writes)
        n = self.count[eng]
        self.count[eng] = n + 1
        key = self._key(('c', eng, n // self.EPOCH))
        tok = (key, n % self.EPOCH + 1, eng)
        self.ops[eng].append((waits, fn, key, 1))
        self._commit(tok, reads, writes)
        return tok

    def dma(self, eng, fn, reads=(), writes=(), side=None, n=1):
        reads = [r.b if isinstance(r, TT) else r for r in reads]
        writes = [w.b if isinstance(w, TT) else w for w in writes]
        side = side.b if isinstance(side, TT) else side
        if side.dkey is None or (side.dcnt + n) * 16 > 30000:
            prev = (side.dkey, side.dcnt * 16, 'dma') if side.dkey is not None and side.dcnt else None
            side.dgen += 1
            newkey = self._key(('d', self.ndk))
            self.ndk += 1
            extra = [prev] if prev else []
            side.dkey = newkey
            side.dcnt = 0
        else:
            extra = [(side.dkey, side.dcnt * 16, 'dma')] if side.dcnt else []
        waits = self._deps(eng, reads, writes, extra)
        side.dcnt += n
        tok = (side.dkey, side.dcnt * 16, 'dma')
        self.ops[eng].append((waits, fn, side.dkey, 16))
        self._commit(tok, reads, writes)
        self.dma_tokens[side.dkey] = side.dcnt * 16
        return tok

    def emit(self):
        nc = self.nc
        eobj = {'pe': nc.tensor, 'act': nc.scalar, 'dve': nc.vector, 'pool': nc.gpsimd, 'sp': nc.sync}
        fin = [(k, v) for k, v in self.dma_tokens.items()]
        with ExitStack() as st:
            sems = {}
            for i, k in enumerate(self.keys):
                sems[k] = st.enter_context(nc.semaphore('%s_s%d' % (self.name, i)))
            with nc.Block() as blk:
                @blk.sync
                def _(e):
                    for k in self.keys:
                        e.sem_clear(sems[k])
            with nc.Block() as blk:
                def mk(engname):
                    def body(e):
                        for waits, fn, key, amt in self.ops[engname]:
                            for (k, v) in waits:
                                e.wait_ge(sems[k], v)
                            r = fn(e)
                            if isinstance(r, (list, tuple)):
                                for ins in r:
                                    ins.then_inc(sems[key], amt)
                            else:
                                r.then_inc(sems[key], amt)
                        if engname == 'sp':
                            for (k, v) in fin:
                                e.wait_ge(sems[k], v)
                    return body
                blk.tensor(mk('pe'))
                blk.scalar(mk('act'))
                blk.vector(mk('dve'))
                blk.gpsimd(mk('pool'))
                blk.sync(mk('sp'))
        self.stack.close()
        return sum(self.count.values())


def fix(ap):
    return ap


OFF = {}
_names = ('a_q', 'a_k', 'a_v', 'b_q', 'b_k', 'b_v', 'c_q', 'c_f_fwd', 'c_f_bwd', 'c_i', 'c_g',
          'd_q', 'd_k', 'd_v', 'd_gk_fwd', 'd_gk_bwd', 'd_g', 'merge')
_sizes = (512, 128, 128, 512, 512, 512, 512, 512, 512, 512, 512, 256, 256, 512, 16, 16, 512, 4096)
_o = 0
for _n, _s in zip(_names, _sizes):
    OFF[_n] = (_o, _s)
    _o += _s
IN_WIDTH = _o

FM_BLOCKS = []
for i in range(4):
    FM_BLOCKS.append(('a_q', i * 128, 128, 'AQT', i * 128, 'rope'))
for i in range(4):
    FM_BLOCKS.append(('a_q', i * 128, 128, None, 0, 'swap'))
FM_BLOCKS.append(('a_k', 0, 128, 'AKT', 0, 'rope'))
FM_BLOCKS.append(('a_k', 0, 128, None, 0, 'swap'))
for i in range(4):
    FM_BLOCKS.append(('b_q', i * 128, 128, 'BQT', i * 128, 'copy'))
for i in range(4):
    FM_BLOCKS.append(('b_k', i * 128, 128, 'BKT', i * 128, 'copy'))
for i in range(4):
    FM_BLOCKS.append(('c_q', i * 128, 128, 'CQT', i * 128, 'silu'))
for i in range(4):
    FM_BLOCKS.append(('c_f_fwd', i * 128, 128, 'CFT', i * 128, 'sigmoid'))
for i in range(4):
    FM_BLOCKS.append(('c_f_bwd', i * 128, 128, 'CFT', 512 + i * 128, 'sigmoid'))
for i in range(2):
    FM_BLOCKS.append(('d_q', i * 128, 128, 'DQT', i * 128, 'copy'))
for i in range(2):
    FM_BLOCKS.append(('d_k', i * 128, 128, 'DKT', i * 128, 'copy'))
FM_BLOCKS.append(('d_gk_fwd', 0, 16, 'DGT', 0, 'copy'))
FM_BLOCKS.append(('d_gk_bwd', 0, 16, 'DGT', 16, 'copy'))
NFM = len(FM_BLOCKS)
TM_BLOCKS = [('a_v', 0, 128, 'AV', 0, 'copy'), ('b_v', 0, 512, 'BV', 0, 'copy'),
             ('c_i', 0, 512, 'CI', 0, 'copy'), ('c_g', 0, 512, 'CG', 0, 'silu'),
             ('d_v', 0, 512, 'DV', 0, 'copy'), ('d_g', 0, 512, 'DG', 0, 'silu')]
for i in range(8):
    TM_BLOCKS.append(('merge', i * 512, 512, 'MG', i * 512, 'sigmoid'))
NTM = len(TM_BLOCKS)

Z_SCRATCH = {
    'AQT': ([NB, 512, T], BF16), 'AKT': ([NB, 128, T], BF16), 'AV': ([NB, T, 128], BF16),
    'BQT': ([NB, 512, T], BF16), 'BKT': ([NB, 512, T], BF16), 'BV': ([NB, T, 512], BF16),
    'CQT': ([NB, 512, T], F32), 'CFT': ([NB, 1024, T], F32), 'CI': ([NB, T, 512], BF16),
    'CG': ([NB, T, 512], BF16), 'DQT': ([NB, 256, T], F32), 'DKT': ([NB, 256, T], F32),
    'DGT': ([NB, 32, T], F32), 'DV': ([NB, T, 512], BF16), 'DG': ([NB, T, 512], BF16),
    'MG': ([NB, T, 4096], BF16),
}
TCHUNKS = [(0, 4), (4, 8), (8, 12), (12, 16), (16, 18)]
GROUPS = [(0, 256), (256, 512), (768, 512), (1280, 512), (1792, 512)]


def rope_partner():
    p = np.zeros(64, dtype=np.int64)
    for d in range(64):
        p[d] = d + 16 if (d % 32) < 16 else d - 16
    return p


def host_w_in_blocks(w_in_l):
    part = rope_partner()
    fm = np.zeros((NFM, 128, 8, 128), dtype=np.float32)
    for bi, (fam, co, ncols, dest, do, ep) in enumerate(FM_BLOCKS):
        o, _ = OFF[fam]
        cols = np.arange(o + co, o + co + ncols)
        if ep == 'swap':
            hd = (cols - o) // 64
            dd = (cols - o) % 64
            cols = o + hd * 64 + part[dd]
        blk = w_in_l[:, cols]
        fm[bi, :, :, :ncols] = blk.reshape(8, 128, ncols).transpose(1, 0, 2)
    tm = np.zeros((NTM, 128, 8, 512), dtype=np.float32)
    for bi, (fam, co, ncols, dest, do, ep) in enumerate(TM_BLOCKS):
        o, _ = OFF[fam]
        blk = w_in_l[:, o + co:o + co + ncols]
        tm[bi, :, :, :ncols] = blk.reshape(8, 128, ncols).transpose(1, 0, 2)
    return fm, tm


def host_rope_tables():
    t = np.arange(SEQ)
    row = (t // 64).astype(np.float32)
    col = (t % 64).astype(np.float32)
    half = 32
    inv = (10000.0 ** (-np.arange(0, half, 2, dtype=np.float32) / half)).astype(np.float32)
    ang_r = row[:, None] * inv
    ang_c = col[:, None] * inv
    cos = np.ones((64, T), dtype=np.float32)
    sin = np.zeros((64, T), dtype=np.float32)
    for d in range(64):
        ang = ang_r if d < 32 else ang_c
        j = d % 16
        sgn = -1.0 if (d % 32) < 16 else 1.0
        cos[d, CTX:] = np.cos(ang[:, j])
        sin[d, CTX:] = sgn * np.sin(ang[:, j])
    return np.concatenate([cos, cos], 0), np.concatenate([sin, sin], 0)


def phase_mod(nc, l, cT, w_mod, bmod3, ng3, MODV):
    ph = Phase(nc, 'p0_%d' % l)
    ct = ph.sb('ct', [128, 8, 3], F32)
    sct = ph.sb('sct', [128, 8, 3], F32)
    bm = ph.sb('bm', [3, 6 * D], F32)
    ng = ph.sb('ng', [3, 4 * D], F32)
    mod = ph.sb('mod', [3, 6 * D], F32)
    outv = ph.sb('outv', [3, 6 * D], F32)
    wb = [ph.sb('wb%d' % i, [128, 8, 512], F32) for i in range(2)]
    pp = [ph.ps('pp%d' % i, [128, 512]) for i in range(2)]
    ph.dma('sp', lambda e: e.dma_start(out=ct[:], in_=cT), writes=[ct], side=ct)
    ph.dma('sp', lambda e: e.dma_start(out=bm[:], in_=bmod3[l]), writes=[bm], side=bm)
    ph.dma('sp', lambda e: e.dma_start(out=ng[:], in_=ng3[l]), writes=[ng], side=ng)
    ph.op('act', lambda e: e.activation(out=sct[:], in_=ct[:], func=AF.Silu), reads=[ct], writes=[sct])
    for j in range(12):
        w = wb[j % 2]
        p = pp[j % 2]
        src = w_mod[l, :, j * 512:(j + 1) * 512].rearrange('(k p) c -> p k c', p=128)
        ph.dma('sp' if j % 2 == 0 else 'act', lambda e, w=w, src=src: e.dma_start(out=w[:], in_=src), writes=[w], side=w)
        for k in range(8):
            ph.op('pe', lambda e, p=p, w=w, k=k: e.matmul(p[0:3, :], lhsT=sct[:, k, :], rhs=w[:, k, :],
                                                            start=(k == 0), stop=(k == 7)),
                  reads=[sct, w], writes=[p])
        ph.op('dve', lambda e, p=p, j=j: e.tensor_tensor(out=mod[:, j * 512:(j + 1) * 512], in0=p[0:3, :],
                                                         in1=bm[:, j * 512:(j + 1) * 512], op=ALU.add),
              reads=[p, bm], writes=[mod])
    def seg(t, k):
        return t[:, k * D:(k + 1) * D]
    ph.op('dve', lambda e: e.scalar_tensor_tensor(out=seg(outv, 0), in0=seg(mod, 1), scalar=1.0, in1=seg(ng, 0),
                                                  op0=ALU.add, op1=ALU.mult), reads=[mod, ng], writes=[outv])
    ph.op('dve', lambda e: e.tensor_copy(out=seg(outv, 1), in_=seg(mod, 0)), reads=[mod], writes=[outv])
    ph.op('dve', lambda e: e.tensor_tensor(out=seg(outv, 2), in0=seg(mod, 2), in1=seg(ng, 1), op=ALU.mult),
          reads=[mod, ng], writes=[outv])
    ph.op('dve', lambda e: e.scalar_tensor_tensor(out=seg(outv, 3), in0=seg(mod, 4), scalar=1.0, in1=seg(ng, 2),
                                                  op0=ALU.add, op1=ALU.mult), reads=[mod, ng], writes=[outv])
    ph.op('dve', lambda e: e.tensor_copy(out=seg(outv, 4), in_=seg(mod, 3)), reads=[mod], writes=[outv])
    ph.op('dve', lambda e: e.tensor_tensor(out=seg(outv, 5), in0=seg(mod, 5), in1=seg(ng, 3), op=ALU.mult),
          reads=[mod, ng], writes=[outv])
    ph.dma('sp', lambda e: e.dma_start(out=MODV, in_=outv[:]), reads=[outv], side=outv)
    return ph.emit()


def bcast_load(ph, eng, dst, MODV, r, k):
    src = MODV[r:r + 1, k * D:(k + 1) * D].partition_broadcast(128)
    ph.dma(eng, lambda e: e.dma_start(out=dst[:], in_=src), writes=[dst], side=dst)


def norm_tile(ph, xt, junk, ss, eng_sq='act'):
    srcs = xt if isinstance(xt, (list, tuple)) else [xt]
    w = D // len(srcs)
    for k, sx in enumerate(srcs):
        col = 0 if k == 0 else 2
        ph.op('act', lambda e, sx=sx, k=k, col=col: e.activation(out=junk[:, k * w:(k + 1) * w], in_=sx(), func=AF.Square,
                                                               accum_out=ss[:, col:col + 1]),
              reads=sx.reads, writes=[junk, ss])
    if len(srcs) > 1:
        ph.op('dve', lambda e: e.tensor_tensor(out=ss[:, 0:1], in0=ss[:, 0:1], in1=ss[:, 2:3], op=ALU.add), reads=[ss], writes=[ss])
    ph.op('act', lambda e: e.activation(out=ss[:, 1:2], in_=ss[:, 0:1], func=AF.Sqrt, scale=1.0 / D, bias=ph.eps[:]),
          reads=[ss, ph.eps], writes=[ss])
    ph.op('dve', lambda e: e.reciprocal(out=ss[:, 1:2], in_=ss[:, 1:2]), reads=[ss], writes=[ss])


class Src:
    def __init__(self, fn, reads):
        self.fn = fn
        self.reads = reads

    def __call__(self):
        return self.fn()


def make_eps(ph):
    ph.eps = ph.sb('eps', [128, 1], F32)
    ph.op('pool', lambda e: e.memset(ph.eps[:], EPS), writes=[ph.eps])


def phase_hT(nc, l, xsrc, MODV, ident_d, hT, kA, ksh):
    ph = Phase(nc, 'hT%d_%d' % (kA, l))
    make_eps(ph)
    idt = ph.sb('idt', [128, 128], F32)
    ph.dma('sp', lambda e: e.dma_start(out=idt[:], in_=ident_d), writes=[idt], side=idt)
    Ab = [ph.sb('Ab%d' % i, [128, D], F32) for i in range(2)]
    shb = [ph.sb('shb%d' % i, [128, D], F32) for i in range(2)]
    bcast_load(ph, 'sp', Ab[0], MODV, 2, kA)
    bcast_load(ph, 'sp', shb[0], MODV, 2, ksh)
    xt = [ph.sb('xt%d' % i, [128, D], F32) for i in range(3)]
    junk = ph.sb('junk', [128, D], F32)
    ss = [ph.sb('ss%d' % i, [128, 4], F32) for i in range(3)]
    hh = [ph.sb('hh%d' % i, [128, D], F32) for i in range(2)]
    pt = [ph.ps('pt%d' % i, [128, 4, 128]) for i in range(4)]
    n = 0
    for b in range(NB):
        bcast_load(ph, 'sp', Ab[1], MODV, b, kA)
        bcast_load(ph, 'sp', shb[1], MODV, b, ksh)
        for i in range(NT):
            x = xt[n % 3]
            s = ss[n % 3]
            h = hh[n % 2]
            ri = 0 if i < 2 else 1
            ph.dma('sp', lambda e, x=x, b=b, i=i: e.dma_start(out=x[:], in_=xsrc[b, i * 128:(i + 1) * 128, :]),
                   writes=[x], side=x)
            norm_tile(ph, Src(lambda x=x: x[:], [x]), junk, s)
            ph.op('dve', lambda e, h=h, x=x, s=s, ri=ri: e.scalar_tensor_tensor(
                out=h[:], in0=x[:], scalar=s[:, 1:2], in1=Ab[ri][:], op0=ALU.mult, op1=ALU.mult),
                reads=[x, s, Ab[ri]], writes=[h])
            ph.op('pool', lambda e, h=h, ri=ri: e.tensor_tensor(out=h[:], in0=h[:], in1=shb[ri][:], op=ALU.add),
                  reads=[h, shb[ri]], writes=[h])
            for half in range(2):
                p = pt[(2 * n + half) % 4]
                for k in range(4):
                    kk = half * 4 + k
                    ph.op('pe', lambda e, p=p, h=h, k=k, kk=kk: e.transpose(
                        out=p[:, k, :], in_=h[:, kk * 128:(kk + 1) * 128], identity=idt[:]),
                        reads=[h, idt], writes=[p])
                dst = lambda b=b, i=i, half=half: hT[:, b, half * 4:(half + 1) * 4, i * 128:(i + 1) * 128]
                if half == 0:
                    ph.op('act', lambda e, p=p, dst=dst: e.copy(out=dst(), in_=p[:]), reads=[p], writes=[hT])
                else:
                    ph.op('dve', lambda e, p=p, dst=dst: e.tensor_copy(out=dst(), in_=p[:]), reads=[p], writes=[hT])
            n += 1
    return ph.emit()


def phase_zproj(nc, l, hT, wfm, wtm, cos_d, sin_d, Z):
    ph = Phase(nc, 'zp_%d' % l)
    cosT = ph.sb('cos', [128, T], F32)
    sinT = ph.sb('sin', [128, T], F32)
    ph.dma('sp', lambda e: e.dma_start(out=cosT[:], in_=cos_d), writes=[cosT], side=cosT)
    ph.dma('sp', lambda e: e.dma_start(out=sinT[:], in_=sin_d), writes=[sinT], side=sinT)
    wf = [ph.sb('wf%d' % i, [128, 8, 128], BF16) for i in range(4)]
    wt = [ph.sb('wt%d' % i, [128, 8, 512], BF16) for i in range(2)]
    pp = [ph.ps('pp%d' % i, [128, 512]) for i in range(6)]
    t1 = [ph.sb('t1_%d' % i, [128, 512], F32) for i in range(2)]
    t2 = [ph.sb('t2_%d' % i, [128, 512], F32) for i in range(2)]
    ob = [ph.sb('ob%d' % i, [128, 512], BF16) for i in range(4)]
    of = [ph.sb('of%d' % i, [128, 512], F32) for i in range(4)]
    cnt = {'w': 0, 'p': 0, 'o': 0, 'e': 0, 'r': 0}

    def nextp():
        p = pp[cnt['p'] % 6]
        cnt['p'] += 1
        return p

    def loadw(bi):
        w = wf[cnt['w'] % 4]
        cnt['w'] += 1
        ph.dma('pool', lambda e, w=w, bi=bi: e.dma_start(out=w[:], in_=wfm[l, bi]), writes=[w], side=w)
        return w

    def mm(p, w, M, b, t0, n):
        for k in range(8):
            ph.op('pe', lambda e, p=p, w=w, k=k, M=M, b=b, t0=t0, n=n: e.matmul(
                p[0:M, 0:n], lhsT=w[:, k, 0:M], rhs=hT[:, b, k, t0:t0 + n], start=(k == 0), stop=(k == 7)),
                reads=[w, hT], writes=[p])

    bi = 0
    while bi < NFM:
        fam, co, M, dest, do, ep = FM_BLOCKS[bi]
        if ep == 'rope':
            nb = 4 if fam == 'a_q' else 1
            for j in range(nb):
                w1 = loadw(bi + j)
                w2 = loadw(bi + nb + j)
                fam, co, M, dest, do, ep = FM_BLOCKS[bi + j]
                dt, dsh = Z[dest]
                for b in range(NB):
                    for (t0, n) in GROUPS:
                        p1 = nextp()
                        p2 = nextp()
                        mm(p1, w1, M, b, t0, n)
                        mm(p2, w2, M, b, t0, n)
                        a = t1[cnt['r'] % 2]
                        c = t2[cnt['r'] % 2]
                        cnt['r'] += 1
                        o = ob[cnt['o'] % 4]
                        cnt['o'] += 1
                        ph.op('dve', lambda e, a=a, p1=p1, t0=t0, n=n: e.tensor_tensor(
                            out=a[:, 0:n], in0=p1[:, 0:n], in1=cosT[:, t0:t0 + n], op=ALU.mult),
                            reads=[p1, cosT], writes=[a])
                        ph.op('dve', lambda e, c=c, p2=p2, t0=t0, n=n: e.tensor_tensor(
                            out=c[:, 0:n], in0=p2[:, 0:n], in1=sinT[:, t0:t0 + n], op=ALU.mult),
                            reads=[p2, sinT], writes=[c])
                        ph.op('pool', lambda e, a=a, c=c, o=o, n=n: e.tensor_tensor(
                            out=o[:, 0:n], in0=a[:, 0:n], in1=c[:, 0:n], op=ALU.add), reads=[a, c], writes=[o])
                        ph.dma('sp', lambda e, o=o, dt=dt, do=do, b=b, t0=t0, n=n: e.dma_start(
                            out=dt[b, do:do + 128, t0:t0 + n], in_=o[:, 0:n]), reads=[o], side=o)
            bi += 2 * nb
            continue
        w = loadw(bi)
        dt, dsh = Z[dest]
        isf32 = (dsh == F32)
        for b in range(NB):
            for (t0, n) in GROUPS:
                p = nextp()
                mm(p, w, M, b, t0, n)
                if isf32:
                    o = of[cnt['o'] % 4]
                else:
                    o = ob[cnt['o'] % 4]
                cnt['o'] += 1
                if ep == 'copy':
                    if cnt['e'] % 2 == 0:
                        ph.op('act', lambda e, o=o, p=p, M=M, n=n: e.copy(out=o[0:M, 0:n], in_=p[0:M, 0:n]),
                              reads=[p], writes=[o])
                    else:
                        ph.op('dve', lambda e, o=o, p=p, M=M, n=n: e.tensor_copy(out=o[0:M, 0:n], in_=p[0:M, 0:n]),
                              reads=[p], writes=[o])
                    cnt['e'] += 1
                else:
                    fn = AF.Silu if ep == 'silu' else AF.Sigmoid
                    ph.op('act', lambda e, o=o, p=p, M=M, n=n, fn=fn: e.activation(
                        out=o[0:M, 0:n], in_=p[0:M, 0:n], func=fn), reads=[p], writes=[o])
                ph.dma('sp', lambda e, o=o, dt=dt, do=do, b=b, t0=t0, n=n, M=M: e.dma_start(
                    out=dt[b, do:do + M, t0:t0 + n], in_=o[0:M, 0:n]), reads=[o], side=o)
        bi += 1
    for bi, (fam, co, ncols, dest, do, ep) in enumerate(TM_BLOCKS):
        w = wt[bi % 2]
        ph.dma('pool', lambda e, w=w, bi=bi: e.dma_start(out=w[:], in_=wtm[l, bi]), writes=[w], side=w)
        dt, dsh = Z[dest]
        for b in range(NB):
            for i in range(NT):
                p = nextp()
                for k in range(8):
                    ph.op('pe', lambda e, p=p, w=w, k=k, b=b, i=i, ncols=ncols: e.matmul(
                        p[:, 0:ncols], lhsT=hT[:, b, k, i * 128:(i + 1) * 128], rhs=w[:, k, 0:ncols],
                        start=(k == 0), stop=(k == 7)), reads=[w, hT], writes=[p])
                o = ob[cnt['o'] % 4]
                cnt['o'] += 1
                if ep == 'copy':
                    ph.op('dve', lambda e, o=o, p=p, ncols=ncols: e.tensor_copy(out=o[:, 0:ncols], in_=p[:, 0:ncols]),
                          reads=[p], writes=[o])
                else:
                    fn = AF.Silu if ep == 'silu' else AF.Sigmoid
                    ph.op('act', lambda e, o=o, p=p, ncols=ncols, fn=fn: e.activation(
                        out=o[:, 0:ncols], in_=p[:, 0:ncols], func=fn), reads=[p], writes=[o])
                ph.dma('sp', lambda e, o=o, dt=dt, do=do, b=b, i=i, ncols=ncols: e.dma_start(
                    out=dt[b, i * 128:(i + 1) * 128, do:do + ncols], in_=o[:, 0:ncols]), reads=[o], side=o)
    return ph.emit()


def build(layers=(0, 1, 2, 3), generic=False, stop_after=None, dbg=()):
    nc = bass.Bass("TRN2", target_bir_lowering=False)
    LD = 1 if generic else DEPTH

    def din(name, shape, dt=F32):
        return nc.dram_tensor(name, list(shape), dt, kind="ExternalInput").ap()

    def dscr(name, shape, dt):
        kind = "ExternalOutput" if name in dbg else "Internal"
        return nc.dram_tensor(name, list(shape), dt, kind=kind).ap()

    I = {}
    I['xs'] = din('xs', [NB, T, D])
    I['cT'] = din('cT', [128, 8, 3])
    I['w_mod'] = din('w_mod', [LD, D, 6 * D])
    I['bmod3'] = din('bmod3', [LD, 3, 6 * D])
    I['ng3'] = din('ng3', [LD, 3, 4 * D])
    I['wfm'] = din('wfm', [LD, NFM, 128, 8, 128])
    I['wtm'] = din('wtm', [LD, NTM, 128, 8, 512])
    I['cosT'] = din('cosT', [128, T])
    I['sinT'] = din('sinT', [128, T])
    I['ident'] = din('ident', [128, 128])
    I['a_sink'] = din('a_sink', [LD, 8])
    I['masksA'] = din('masksA', [128, 2, 512])
    I['biasB'] = din('biasB', [LD, len(B_CASES), 128, 8, 128])
    I['WB'] = din('WB', [LD, 128, 16, D])
    I['WO'] = din('WO', [LD, 128, 8, D])
    I['W1'] = din('W1', [LD, 128, 8, DFF])
    I['W2'] = din('W2', [LD, 128, 32, D])
    I['clbT'] = din('clbT', [128, 4, DEPTH])
    I['lsel'] = din('lsel', [LD, 128, DEPTH])
    I['masksS'] = din('masksS', [128, 2, 128])
    I['cnorm'] = din('cnorm', [LD, 128])
    I['dnorm'] = din('dnorm', [LD, 128])
    I['dup'] = din('dup', [LD, 2, 16, 256])
    I['dbiasT'] = din('dbiasT', [LD, 128, 2, 2])
    out = nc.dram_tensor('out', [NB, SEQ, D], F32, kind="ExternalOutput").ap()
    MODV = dscr('MODV', [3, 6 * D], F32)
    Z = {k: (dscr(k, sh, dt), dt) for k, (sh, dt) in Z_SCRATCH.items()}
    Y = dscr('Y', [NB, T, 4, 512], BF16)
    XA = dscr('XA', [NB, T, D], F32)
    if generic:
        XB = nc.dram_tensor('XB', [NB, T, D], F32, kind="ExternalOutput").ap()
    else:
        XB = dscr('XB', [NB, T, D], F32)
    AT = dscr('AT', [NB, DFF, T], BF16)
    ninst = 0
    for li, lay in enumerate(layers):
        l = 0 if generic else lay
        last = (lay == DEPTH - 1)
        xsrc = I['xs'] if (li == 0) else XB
        ninst += phase_mod(nc, l, I['cT'], I['w_mod'], I['bmod3'], I['ng3'], MODV)
        if stop_after == 'mod':
            break
        with nc.sbuf_tensor('hT_%d' % l, [128, NB, 8, T], BF16) as hT_t:
            ninst += phase_hT(nc, l, xsrc, MODV, I['ident'], TT(hT_t, Buf('hT')), 0, 1)
            ninst += phase_zproj(nc, l, TT(hT_t, Buf('hT')), I['wfm'], I['wtm'], I['cosT'], I['sinT'], Z)
        if stop_after == 'zproj':
            break
        if 'a' not in SKIP:
            ninst += phase_attn_a(nc, l, Z, I['a_sink'], I['masksA'], Y)
        if 'b' not in SKIP:
            ninst += phase_attn_b(nc, l, Z, I['biasB'], Y)
        if stop_after == 'ab':
            break
        if 'c' not in SKIP:
            ninst += phase_scan(nc, l, 'c', Z, I, Y)
        if 'd' not in SKIP:
            ninst += phase_scan(nc, l, 'd', Z, I, Y)
        if stop_after == 'mix':
            break
        if 'ffn' in SKIP:
            continue
        ninst += phase_merge(nc, l, xsrc, Y, Z['MG'][0], I['WB'], I['WO'], MODV, I['ident'], XA)
        if stop_after == 'merge':
            break
        with nc.sbuf_tensor('h2T_%d' % l, [128, NB, 8, T], BF16) as hT_t:
            ninst += phase_hT(nc, l + 100, XA, MODV, I['ident'], TT(hT_t, Buf('hT')), 3, 4)
            ninst += phase_ffn_up(nc, l, TT(hT_t, Buf('hT')), I['W1'], AT)
        ninst += phase_ffn_down(nc, l, XA, AT, I['W2'], MODV, XB, out, both=generic, last=last)
    return nc, ninst


def host_prep(inputs, core):
    b0 = core * NB
    m = {}
    m['xs'] = np.ascontiguousarray(np.concatenate([inputs['ctx'][b0:b0 + NB], inputs['x'][b0:b0 + NB]], axis=1))
    rows = np.stack([inputs['c'][b0], inputs['c'][b0 + 1], inputs['c_ctx']], 0)
    m['cT'] = np.ascontiguousarray(rows.reshape(3, 8, 128).transpose(2, 1, 0))
    return m


def host_shared(inputs):
    f = np.float32
    s = {}
    s['w_mod'] = np.ascontiguousarray(inputs['w_mod'])
    s['bmod3'] = np.ascontiguousarray(np.repeat(inputs['b_mod'][:, None, :], 3, axis=1))
    s['ng3'] = np.ascontiguousarray(np.repeat(inputs['norm_gains'].reshape(DEPTH, 1, 4 * D), 3, axis=1))
    fm = np.zeros((DEPTH, NFM, 128, 8, 128), f)
    tm = np.zeros((DEPTH, NTM, 128, 8, 512), f)
    for l in range(DEPTH):
        fm[l], tm[l] = host_w_in_blocks(inputs['w_in'][l])
    s['wfm'] = fm
    s['wtm'] = tm
    c, sn = host_rope_tables()
    s['cosT'] = c
    s['sinT'] = sn
    s['ident'] = np.eye(128, dtype=f)
    s['a_sink'] = np.ascontiguousarray(inputs['a_sink'])
    j = np.arange(128)
    mp = (j[:, None] >= j[None, :]).astype(f)
    mn = (j[:, None] <= j[None, :]).astype(f)
    s['masksA'] = np.ascontiguousarray(np.stack([np.tile(mp, (1, 4)), np.tile(mn, (1, 4))], 1))
    s['biasB'] = np.stack([host_bias_b(inputs['b_rel_bias'][l]) for l in range(DEPTH)], 0)
    s['WB'] = np.ascontiguousarray(inputs['w_branch'].reshape(DEPTH, 16, 128, D).transpose(0, 2, 1, 3))
    s['WO'] = np.ascontiguousarray(inputs['w_out'].reshape(DEPTH, 8, 128, D).transpose(0, 2, 1, 3))
    s['W1'] = np.ascontiguousarray(inputs['w_ff1'].reshape(DEPTH, 8, 128, DFF).transpose(0, 2, 1, 3))
    s['W2'] = np.ascontiguousarray(inputs['w_ff2'].reshape(DEPTH, 32, 128, D).transpose(0, 2, 1, 3))
    s['clbT'] = np.ascontiguousarray(inputs['c_lower_bounds'].reshape(DEPTH, 4, 128).transpose(2, 1, 0))
    same = (j[:, None] // 32) == (j[None, :] // 32)
    mf = (same & (j[:, None] <= j[None, :])).astype(f)
    mb = (same & (j[:, None] >= j[None, :])).astype(f)
    s['masksS'] = np.ascontiguousarray(np.stack([mf, mb], 1))
    ls = np.zeros((DEPTH, 128, DEPTH), f)
    for l in range(DEPTH):
        ls[l, :, 1:l + 1] = 1.0
    s['lsel'] = ls
    s['cnorm'] = np.ascontiguousarray(inputs['c_norm'])
    s['dnorm'] = np.ascontiguousarray(inputs['d_norm'])
    s['dup'] = np.ascontiguousarray(inputs['d_gate_up'])
    s['dbiasT'] = np.ascontiguousarray(inputs['d_gate_bias'].reshape(DEPTH, 2, 2, 128).transpose(0, 3, 1, 2))
    return s


def phase_scan(nc, l, which, Z, I, Y):
    isC = which == 'c'
    ph = Phase(nc, 's%s_%d' % (which, l))
    make_eps(ph)
    nblk = 4 if isC else 2
    hpb = 1 if isC else 2
    dk = 128 // hpb
    jbr = 2 if isC else 3
    Vd = Z['CI' if isC else 'DV'][0]
    Gd = Z['CG' if isC else 'DG'][0]
    NCH = T // 32
    one = ph.sb('one', [128, 1], F32)
    ph.op('pool', lambda e: e.memset(one[:], 1.0), writes=[one])
    idb = ph.sb('idb', [128, 128], BF16)
    ph.dma('pool', lambda e: e.dma_start(out=idb[:], in_=I['ident']), writes=[idb], side=idb)
    msk = ph.sb('msk', [128, 2, 128], BF16)
    ph.dma('pool', lambda e: e.dma_start(out=msk[:], in_=I['masksS']), writes=[msk], side=msk)
    gn = ph.sb('gn', [128, 128], F32)
    gsrc = I['cnorm' if isC else 'dnorm']
    ph.dma('sp', lambda e: e.dma_start(out=gn[:], in_=gsrc[l:l + 1, :].partition_broadcast(128)), writes=[gn], side=gn)
    rm = ph.sb('rm', [128, T], BF16)
    ph.op('pool', lambda e: e.memset(rm[:], 1.0), writes=[rm])
    ph.op('pool', lambda e: e.memset(rm[:].rearrange('p (c s) -> p c s', s=32)[:, :, 0:1], 0.0), writes=[rm])
    zmask = ph.sb('zmask', [128, T], BF16)
    ph.op('pool', lambda e: e.memset(zmask[:], 1.0), writes=[zmask])
    ph.op('pool', lambda e: e.memset(zmask[:].rearrange('p (i c s) -> p i c s', c=4, s=32)[:, :, 2, :], 0.0), writes=[zmask])
    cmask = ph.sb('cmask', [128, 4], F32)
    ph.op('pool', lambda e: e.memset(cmask[:], 0.0), writes=[cmask])
    ph.op('pool', lambda e: e.memset(cmask[0:32, 0:1], 1.0), writes=[cmask])
    ph.op('pool', lambda e: e.memset(cmask[32:64, 1:2], 1.0), writes=[cmask])
    ph.op('pool', lambda e: e.memset(cmask[64:128, 3:4], 1.0), writes=[cmask])
    ph.op('pool', lambda e: e.memset(cmask[64:96, 2:3], 1.0), writes=[cmask])
    ph.op('pool', lambda e: e.memset(cmask[64:96, 3:4], 0.0), writes=[cmask])
    hmask = ph.sb('hmask', [128, 2], F32)
    ph.op('pool', lambda e: e.memset(hmask[:], 0.0), writes=[hmask])
    ph.op('pool', lambda e: e.memset(hmask[0:64, 0:1], 1.0), writes=[hmask])
    ph.op('pool', lambda e: e.memset(hmask[64:128, 1:2], 1.0), writes=[hmask])
    rowmask = ph.sb('rowmask', [128, 1], F32)
    ph.op('pool', lambda e: e.memset(rowmask[:], 1.0), writes=[rowmask])
    ph.op('pool', lambda e: e.memset(rowmask[64:96, :], 0.0), writes=[rowmask])
    if isC:
        clb = ph.sb('clb', [128, 4, DEPTH], F32)
        lb = ph.sb('lb', [128, 4], F32)
        oml = ph.sb('oml', [128, 4], F32)
        ssum = ph.sb('ssum', [128, 4], F32)
        ph.dma('sp', lambda e: e.dma_start(out=clb[:], in_=I['clbT']), writes=[clb], side=clb)
        ph.op('act', lambda e: e.activation(out=clb[:], in_=clb[:], func=AF.Exp), reads=[clb], writes=[clb])
        ph.op('dve', lambda e: e.tensor_tensor(out=ssum[:], in0=clb[:, :, 0], in1=clb[:, :, 1], op=ALU.add), reads=[clb], writes=[ssum])
        for q in range(2, DEPTH):
            ph.op('dve', lambda e, q=q: e.tensor_tensor(out=ssum[:], in0=ssum[:], in1=clb[:, :, q], op=ALU.add), reads=[clb, ssum], writes=[ssum])
        ph.op('dve', lambda e: e.reciprocal(out=ssum[:], in_=ssum[:]), reads=[ssum], writes=[ssum])
        ph.op('pool', lambda e: e.memset(lb[:], 0.0), writes=[lb])
        lsel = ph.sb('lsel', [128, DEPTH], F32)
        ph.dma('sp', lambda e: e.dma_start(out=lsel[:], in_=I['lsel'][l]), writes=[lsel], side=lsel)
        for q in range(1, DEPTH):
            ph.op('dve', lambda e, q=q: e.scalar_tensor_tensor(out=lb[:], in0=clb[:, :, q], scalar=lsel[:, q:q + 1], in1=lb[:],
                                                               op0=ALU.mult, op1=ALU.add), reads=[clb, lb, lsel], writes=[lb])
        ph.op('dve', lambda e: e.tensor_tensor(out=lb[:], in0=lb[:], in1=ssum[:], op=ALU.mult), reads=[lb, ssum], writes=[lb])
        ph.op('dve', lambda e: e.tensor_scalar(out=oml[:], in0=lb[:], scalar1=-1.0, scalar2=1.0, op0=ALU.mult, op1=ALU.add),
              reads=[lb], writes=[oml])
    else:
        upw = ph.sb('upw', [16, 2, 256], F32)
        negb = ph.sb('negb', [128, 2, 2], F32)
        zdt = ph.sb('zdt', [16, T], F32)
        ph.dma('sp', lambda e: e.dma_start(out=upw[:], in_=I['dup'][l].rearrange('r k c -> k r c')), writes=[upw], side=upw)
        ph.dma('sp', lambda e: e.dma_start(out=negb[:], in_=I['dbiasT'][l]), writes=[negb], side=negb)
        ph.op('dve', lambda e: e.tensor_scalar(out=negb[:], in0=negb[:], scalar1=-1.0, scalar2=None, op0=ALU.mult),
              reads=[negb], writes=[negb])
        psx = ph.ps('psx', [128, 512])
    v = ph.sb('v', [128, NT, 512], BF16)
    og = ph.sb('og', [128, NT, 512], BF16)
    of = ph.sb('of', [128, NT, 512], F32)
    of_b = [Buf('of%d' % i) for i in range(NT)]
    ybuf = ph.sb('ybuf', [128, NT, 512], BF16)
    tA = ph.sb('tA', [128, T], F32)
    tB = ph.sb('tB', [128, T], F32)
    tC = ph.sb('tC', [128, T], F32)
    tD = ph.sb('tD', [128, T], F32)
    tE = ph.sb('tE', [128, T], F32)
    qTs = [ph.sb('qT%d' % i, [128, T], BF16) for i in range(2)]
    kTs = [ph.sb('kT%d' % i, [128, T], BF16) for i in range(2)]
    PAIR = 2 if isC else 1
    qTzs = [ph.sb('qTz%d' % i, [128, T], BF16) for i in range(PAIR)]
    Es = [ph.sb('E%d' % i, [128, NCH], F32) for i in range(2)]
    S32s = [ph.sb('S32_%d' % i, [128, 128], F32) for i in range(PAIR)]
    Sbs = [[ph.sb('Sb%d_%d' % (c_, i), [128, hpb, 128], BF16) for i in range(6)] for c_ in range(PAIR)]
    ktm = [ph.sb('ktm%d' % i, [128, 4, 128], BF16) for i in range(2)]
    kTms = [[ph.sb('kTm%d_%d' % (i, hh), [128, T], BF16) for hh in range(hpb)] for i in range(1)] if hpb > 1 else None
    am = [ph.sb('am%d' % i, [128, hpb, 128], BF16) for i in range(2)]
    osum = [ph.sb('osum%d' % i, [128, hpb, 128], F32) for i in range(2)]
    junk = ph.sb('junk', [128, 128], F32)
    ssq = [ph.sb('ssq%d' % i, [128, 2 * hpb], F32) for i in range(2)]
    tt = [ph.sb('tt%d' % i, [128, 128], F32) for i in range(2)]
    tpk = ph.ps('tpk', [128, 128], BF16)
    nA = 2 if isC else 1
    psA = [ph.ps('psA%d' % i, [128, hpb, 128]) for i in range(nA)]
    psD = [ph.ps('psD%d' % i, [128, 4, 128]) for i in range(nA)]
    pos = [[ph.ps('po%d_%d' % (i, hh), [128, 512]) for hh in range(hpb)] for i in range(2)]
    nset = 0
    ntile = 0
    nsb = 0
    LV = SCAN_DBG.get('level', 9)
    for b in range(SCAN_DBG.get('nb', NB) if LV >= 2 else 0):
        for (i0, i1) in TCHUNKS:
            ph.dma('sp', lambda e, b=b, i0=i0, i1=i1: e.dma_start(
                out=v[:, i0:i1, :], in_=Vd[b, i0 * 128:i1 * 128, :].rearrange('(i p) c -> p i c', p=128)), writes=[v], side=v)
            ph.dma('sp', lambda e, b=b, i0=i0, i1=i1: e.dma_start(
                out=og[:, i0:i1, :], in_=Gd[b, i0 * 128:i1 * 128, :].rearrange('(i p) c -> p i c', p=128)), writes=[og], side=og)
        for dirn in range(2):
            sgn = 1.0 if dirn == 0 else -1.0
            def chain(blk, cs, dirn=dirn, sgn=sgn, b=b):
                qT = qTs[cs]
                qTz = qTzs[cs]
                kT = kTs[cs]
                E = Es[cs]
                S32 = S32s[cs]
                Sb = Sbs[cs]
                nsb = 0
                ntl = 0
                r0 = blk * 128
                if isC:
                    src = Z['CFT'][0][b, dirn * 512 + r0:dirn * 512 + r0 + 128, :]
                    ph.dma('sp', lambda e, src=src: e.dma_start(out=tA[:], in_=src), writes=[tA], side=tA)
                    ph.op('dve', lambda e, blk=blk: e.tensor_scalar(out=tA[:], in0=tA[:], scalar1=oml[:, blk:blk + 1],
                                                                    scalar2=lb[:, blk:blk + 1], op0=ALU.mult, op1=ALU.add),
                          reads=[tA, oml, lb], writes=[tA])
                    ph.op('act', lambda e: e.activation(out=tB[:], in_=tA[:], func=AF.Ln), reads=[tA], writes=[tB])
                    ph.op('pool', lambda e: e.tensor_scalar(out=tA[:], in0=tA[:], scalar1=-1.0, scalar2=1.0,
                                                            op0=ALU.mult, op1=ALU.add), reads=[tA], writes=[tA])
                    qsrc = Z['CQT'][0][b, r0:r0 + 128, :]
                    qscale = 1.0
                else:
                    ph.dma('sp', lambda e, b=b, dirn=dirn: e.dma_start(out=zdt[:], in_=Z['DGT'][0][b, dirn * 16:(dirn + 1) * 16, :]),
                           writes=[zdt], side=zdt)
                    for (t0, n) in GROUPS:
                        ph.op('pe', lambda e, dirn=dirn, r0=r0, t0=t0, n=n: e.matmul(
                            psx[:, 0:n], lhsT=upw[:, dirn, r0:r0 + 128], rhs=zdt[:, t0:t0 + n], start=True, stop=True),
                            reads=[upw, zdt], writes=[psx])
                        ph.op('act', lambda e, dirn=dirn, blk=blk, t0=t0, n=n: e.activation(
                            out=tB[:, t0:t0 + n], in_=psx[:, 0:n], func=AF.Exp, scale=-1.0, bias=negb[:, dirn, blk:blk + 1]),
                            reads=[psx, negb], writes=[tB])
                    ph.op('act', lambda e: e.activation(out=tB[:], in_=tB[:], func=AF.Ln, bias=one[:]), reads=[tB, one], writes=[tB])
                    ph.op('pool', lambda e: e.tensor_scalar(out=tB[:], in0=tB[:], scalar1=-1.0 / 16.0, scalar2=None, op0=ALU.mult),
                          reads=[tB], writes=[tB])
                    ksrc = Z['DKT'][0][b, r0:r0 + 128, :]
                    ph.dma('sp', lambda e, ksrc=ksrc: e.dma_start(out=tA[:], in_=ksrc), writes=[tA], side=tA)
                    qsrc = Z['DQT'][0][b, r0:r0 + 128, :]
                    qscale = 0.125
                ph.dma('sp', lambda e, qsrc=qsrc: e.dma_start(out=tE[:], in_=qsrc), writes=[tE], side=tE)
                ph.op('dve', lambda e: e.tensor_tensor_scan(out=tC[:], data0=rm[:], data1=tB[:], initial=0.0,
                                                            op0=ALU.mult, op1=ALU.add), reads=[rm, tB], writes=[tC])
                ends = lambda t_: t_[:].rearrange('p (c s) -> p c s', s=32)[:, :, 31]
                ph.op('act', lambda e, E=E: e.activation(out=E[:], in_=ends(tC), func=AF.Exp), reads=[tC], writes=[E])
                if dirn == 1:
                    ph.op('pool', lambda e: e.tensor_tensor(out=tC[:], in0=tC[:], in1=tB[:], op=ALU.subtract),
                          reads=[tC, tB], writes=[tC])
                ph.op('act', lambda e, sgn=sgn: e.activation(out=tD[:], in_=tC[:], func=AF.Exp, scale=sgn), reads=[tC], writes=[tD])
                ph.op('act', lambda e, sgn=sgn: e.activation(out=tB[:], in_=tC[:], func=AF.Exp, scale=-sgn), reads=[tC], writes=[tB])
                ph.op('dve', lambda e, qT=qT, qscale=qscale: e.scalar_tensor_tensor(
                    out=qT[:], in0=tE[:], scalar=qscale, in1=tD[:], op0=ALU.mult, op1=ALU.mult), reads=[tE, tD], writes=[qT])
                ph.op('pool', lambda e, kT=kT: e.tensor_tensor(out=kT[:], in0=tA[:], in1=tB[:], op=ALU.mult),
                      reads=[tA, tB], writes=[kT])
                ph.op('pool', lambda e, qT=qT, qTz=qTz: e.tensor_tensor(out=qTz[:], in0=qT[:], in1=zmask[:], op=ALU.mult),
                      reads=[qT, zmask], writes=[qTz])
                if hpb > 1:
                    kTm = kTms[0]
                    for hh in range(hpb):
                        ph.op('dve' if hh == 0 else 'pool', lambda e, kT=kT, kTm=kTm, hh=hh: e.tensor_scalar(
                            out=kTm[hh][:], in0=kT[:], scalar1=hmask[:, hh:hh + 1], scalar2=None, op0=ALU.mult),
                            reads=[kT, hmask], writes=[kTm[hh]])
                else:
                    kTm = [kT]
                yield
                if LV < 3:
                    return
                ph.op('pool', lambda e: e.memset(S32[:], 0.0), writes=[S32])
                scur = Sb[nsb % 6]
                nsb += 1
                ph.op('pool', lambda e, scur=scur: e.memset(scur[:], 0.0), writes=[scur])
                order = list(range(NT)) if dirn == 0 else [1, 0] + list(range(NT - 1, 1, -1))
                corder = [0, 1, 2, 3] if dirn == 0 else [3, 2, 1, 0]
                for ti in order:
                    tsl = slice(ti * 128, (ti + 1) * 128)
                    ix = cs if PAIR == 2 else (ntl % 2)
                    ntl += 1
                    kt_ = ktm[ix]
                    am_ = am[ix]
                    pA = psA[ix % nA]
                    pD = psD[ix % nA]
                    po = pos[ix]
                    os_ = osum[ix]
                    sq_ = ssq[ix]
                    ph.op('pe', lambda e, kT=kT, tsl=tsl: e.transpose(out=tpk[:], in_=kT[:, tsl], identity=idb[:]),
                          reads=[kT, idb], writes=[tpk])
                    for c in range(4):
                        ph.op('act', lambda e, kt_=kt_, c=c: e.activation(out=kt_[:, c, :], in_=tpk[:], func=AF.Identity,
                                                                          scale=cmask[:, c:c + 1]),
                              reads=[tpk, cmask], writes=[kt_])
                    if LV < 4:
                        continue
                    for hh in range(hpb):
                        rs = slice(hh * dk, (hh + 1) * dk)
                        ph.op('pe', lambda e, pA=pA, hh=hh, kTm=kTm, qT=qT, tsl=tsl: e.matmul(
                            pA[:, hh, :], lhsT=kTm[hh][:, tsl], rhs=qT[:, tsl], start=True, stop=True),
                            reads=[kTm[hh], qT], writes=[pA])
                    ph.op('dve', lambda e, pA=pA, am_=am_, dirn=dirn: e.tensor_tensor(
                        out=am_[:], in0=pA[:], in1=msk[:, dirn:dirn + 1, :].to_broadcast([128, hpb, 128]), op=ALU.mult),
                        reads=[pA, msk], writes=[am_])
                    if LV < 5:
                        continue
                    for c in range(4):
                        for hh in range(hpb):
                            rs = slice(hh * dk, (hh + 1) * dk)
                            hd = blk * hpb + hh
                            ph.op('pe', lambda e, pD=pD, c=c, rs=rs, kt_=kt_, ti=ti, hd=hd: e.matmul(
                                pD[rs, c, :], lhsT=kt_[:, c, rs], rhs=v[:, ti, hd * 128:(hd + 1) * 128],
                                start=True, stop=True), reads=[kt_, v], writes=[pD])
                    if LV < 6:
                        continue
                    for hh in range(hpb):
                        hd = blk * hpb + hh
                        ph.op('pe', lambda e, po=po, hh=hh, am_=am_, ti=ti, hd=hd: e.matmul(
                            po[hh][:, 0:128], lhsT=am_[:, hh, :], rhs=v[:, ti, hd * 128:(hd + 1) * 128], start=True, stop=False),
                            reads=[am_, v], writes=[po[hh]])
                    for cn, c in enumerate(corder):
                        gc = ti * 4 + c
                        if dirn == 1:
                            ph.op('dve', lambda e, E=E, gc=gc: e.tensor_scalar(out=S32[:], in0=S32[:], scalar1=E[:, gc:gc + 1],
                                                                             scalar2=None, op0=ALU.mult), reads=[S32, E], writes=[S32])
                            scur = Sb[nsb % 6]
                            nsb += 1
                            if hpb == 1:
                                ph.op('act', lambda e, scur=scur: e.copy(out=scur[:, 0, :], in_=S32[:]), reads=[S32], writes=[scur])
                            else:
                                for hq in range(hpb):
                                    ph.op('act', lambda e, scur=scur, hq=hq: e.activation(
                                        out=scur[:, hq, :], in_=S32[:], func=AF.Identity, scale=hmask[:, hq:hq + 1]),
                                        reads=[S32, hmask], writes=[scur])
                        for hh in range(hpb):
                            rs = slice(hh * dk, (hh + 1) * dk)
                            if c < 3:
                                ph.op('pe', lambda e, po=po, hh=hh, c=c, qT=qT, ti=ti, scur=scur, cn=cn: e.matmul(
                                    po[hh][32 * c:32 * c + 32, 0:128], lhsT=qT[:, ti * 128 + 32 * c:ti * 128 + 32 * c + 32],
                                    rhs=scur[:, hh, :], start=False, stop=(cn == 3)), reads=[qT, scur], writes=[po[hh]])
                            else:
                                ph.op('pe', lambda e, po=po, hh=hh, qTz=qTz, ti=ti, scur=scur, cn=cn: e.matmul(
                                    po[hh][64:128, 0:128], lhsT=qTz[:, ti * 128 + 64:ti * 128 + 128],
                                    rhs=scur[:, hh, :], start=False, stop=(cn == 3)), reads=[qTz, scur], writes=[po[hh]])
                        ph.op('dve', lambda e, pD=pD, c=c: e.tensor_tensor(out=S32[:], in0=S32[:], in1=pD[:, c, :], op=ALU.add),
                              reads=[S32, pD], writes=[S32])
                        if dirn == 0:
                            ph.op('dve', lambda e, E=E, gc=gc: e.tensor_scalar(out=S32[:], in0=S32[:], scalar1=E[:, gc:gc + 1],
                                                                             scalar2=None, op0=ALU.mult), reads=[S32, E], writes=[S32])
                            scur = Sb[nsb % 6]
                            nsb += 1
                            if hpb == 1:
                                ph.op('act', lambda e, scur=scur: e.copy(out=scur[:, 0, :], in_=S32[:]), reads=[S32], writes=[scur])
                            else:
                                for hq in range(hpb):
                                    ph.op('act', lambda e, scur=scur, hq=hq: e.activation(
                                        out=scur[:, hq, :], in_=S32[:], func=AF.Identity, scale=hmask[:, hq:hq + 1]),
                                        reads=[S32, hmask], writes=[scur])
                    if LV < 7:
                        continue
                    for hh in range(hpb):
                        hd = blk * hpb + hh
                        csl = slice(hd * 128, (hd + 1) * 128)
                        if dirn == 0:
                            ph.op('act', lambda e, po=po, hh=hh, ti=ti, csl=csl: e.copy(out=of[:, ti, csl], in_=po[hh][:, 0:128]),
                                  reads=[po[hh]], writes=[of_b[ti]])
                        else:
                            ph.op('dve', lambda e, po=po, hh=hh, ti=ti, csl=csl, os_=os_: e.tensor_tensor(
                                out=os_[:, hh, :], in0=po[hh][:, 0:128], in1=of[:, ti, csl], op=ALU.add),
                                reads=[po[hh], of_b[ti]], writes=[os_])
                            ph.op('act', lambda e, os_=os_, hh=hh, sq_=sq_: e.activation(
                                out=junk[:], in_=os_[:, hh, :], func=AF.Square, accum_out=sq_[:, hh:hh + 1]),
                                reads=[os_], writes=[junk, sq_])
                    if dirn == 1:
                        ph.op('act', lambda e, sq_=sq_: e.activation(out=sq_[:, hpb:2 * hpb], in_=sq_[:, 0:hpb], func=AF.Sqrt,
                                                                     scale=1.0 / 128, bias=ph.eps[:]), reads=[sq_, ph.eps], writes=[sq_])
                        ph.op('dve', lambda e, sq_=sq_: e.reciprocal(out=sq_[:, hpb:2 * hpb], in_=sq_[:, hpb:2 * hpb]),
                              reads=[sq_], writes=[sq_])
                        for hh in range(hpb):
                            hd = blk * hpb + hh
                            csl = slice(hd * 128, (hd + 1) * 128)
                            t_ = tt[hh % 2]
                            ph.op('dve', lambda e, t_=t_, os_=os_, hh=hh, sq_=sq_: e.scalar_tensor_tensor(
                                out=t_[:], in0=os_[:, hh, :], scalar=sq_[:, hpb + hh:hpb + hh + 1], in1=gn[:],
                                op0=ALU.mult, op1=ALU.mult), reads=[os_, sq_, gn], writes=[t_])
                            ph.op('pool', lambda e, t_=t_, ti=ti, csl=csl: e.tensor_tensor(
                                out=ybuf[:, ti, csl], in0=t_[:], in1=og[:, ti, csl], op=ALU.mult), reads=[t_, og], writes=[ybuf])
                    yield
            nb_ = SCAN_DBG.get('nblk', nblk)
            for p0 in range(0, nb_, PAIR):
                gens = [chain(p0 + k_, k_) for k_ in range(PAIR) if p0 + k_ < nb_]
                while gens:
                    for g_ in list(gens):
                        try:
                            next(g_)
                        except StopIteration:
                            gens.remove(g_)
        for (i0, i1) in (TCHUNKS if LV >= 9 else []):
            ph.dma('sp', lambda e, b=b, i0=i0, i1=i1: e.dma_start(
                out=Y[b, i0 * 128:i1 * 128, jbr, :].rearrange('(i p) c -> p i c', p=128), in_=ybuf[:, i0:i1, :]),
                reads=[ybuf], side=ybuf)
    return ph.emit()


def phase_merge(nc, l, xsrc, Y, MG, WB, WO, MODV, identb_d, XA):
    ph = Phase(nc, 'mg_%d' % l)
    make_eps(ph)
    idb = ph.sb('idb', [128, 128], BF16)
    ph.dma('pool', lambda e: e.dma_start(out=idb[:], in_=identb_d), writes=[idb], side=idb)
    wb = ph.sb('wb', [128, 16, D], BF16)
    wo = ph.sb('wo', [128, 8, D], BF16)
    for k in range(16):
        ph.dma('pool', lambda e, k=k: e.dma_start(out=wb[:, k, :], in_=WB[l, :, k, :]), writes=[wb], side=wb)
    for k in range(8):
        ph.dma('pool', lambda e, k=k: e.dma_start(out=wo[:, k, :], in_=WO[l, :, k, :]), writes=[wo], side=wo)
    Gb = [ph.sb('Gb%d' % i, [128, D], F32) for i in range(2)]
    bcast_load(ph, 'sp', Gb[0], MODV, 2, 2)
    yt = [ph.sb('yt%d' % i, [128, 4, 512], BF16) for i in range(2)]
    mg = [ph.sb('mgt%d' % i, [128, 4096], BF16) for i in range(2)]
    xt = [ph.sb('xt%d' % i, [128, D], F32) for i in range(2)]
    yT = [ph.sb('yT%d' % i, [128, 16, 128], BF16) for i in range(2)]
    acc = [ph.sb('acc%d' % i, [128, D], F32) for i in range(2)]
    tmp = [ph.sb('tmp%d' % i, [128, 512], F32) for i in range(2)]
    accb = ph.sb('accb', [128, D], BF16)
    accT = ph.sb('accT', [128, 8, 128], BF16)
    junk = ph.sb('junk', [128, D], F32)
    ss = [ph.sb('ss%d' % i, [128, 4], F32) for i in range(2)]
    xo = [ph.sb('xo%d' % i, [128, D], F32) for i in range(2)]
    tp = [ph.ps('tp%d' % i, [128, 8, 128], BF16) for i in range(2)]
    pj = [ph.ps('pj%d' % i, [128, 512]) for i in range(3)]
    yo = ph.ps('yo', [128, D])
    n = 0
    npj = 0
    for b in range(NB):
        bcast_load(ph, 'sp', Gb[1], MODV, b, 2)
        for i in range(NT):
            ri = 0 if i < 2 else 1
            y_ = yt[n % 2]
            m_ = mg[n % 2]
            x_ = xt[n % 2]
            yT_ = yT[n % 2]
            a_ = acc[n % 2]
            s_ = ss[n % 2]
            o_ = xo[n % 2]
            tsl = slice(i * 128, (i + 1) * 128)
            ph.dma('sp', lambda e, y_=y_, b=b, tsl=tsl: e.dma_start(out=y_[:], in_=Y[b, tsl, :, :]), writes=[y_], side=y_)
            ph.dma('sp', lambda e, m_=m_, b=b, tsl=tsl: e.dma_start(out=m_[:], in_=MG[b, tsl, :]), writes=[m_], side=m_)
            ph.dma('sp', lambda e, x_=x_, b=b, tsl=tsl: e.dma_start(out=x_[:], in_=xsrc[b, tsl, :]), writes=[x_], side=x_)
            for hf in range(2):
                for k in range(8):
                    kk = hf * 8 + k
                    ph.op('pe', lambda e, hf=hf, k=k, kk=kk, y_=y_: e.transpose(
                        out=tp[hf][:, k, :], in_=y_[:, kk // 4, (kk % 4) * 128:(kk % 4 + 1) * 128], identity=idb[:]),
                        reads=[y_, idb], writes=[tp[hf]])
                if hf == 0:
                    ph.op('act', lambda e, yT_=yT_: e.copy(out=yT_[:, 0:8, :], in_=tp[0][:]), reads=[tp[0]], writes=[yT_])
                else:
                    ph.op('dve', lambda e, yT_=yT_: e.tensor_copy(out=yT_[:, 8:16, :], in_=tp[1][:]), reads=[tp[1]], writes=[yT_])
            for j in range(4):
                for c in range(2):
                    p = pj[npj % 3]
                    npj += 1
                    csl = slice(c * 512, (c + 1) * 512)
                    for kk in range(4):
                        ph.op('pe', lambda e, p=p, j=j, kk=kk, csl=csl, yT_=yT_: e.matmul(
                            p[:], lhsT=yT_[:, j * 4 + kk, :], rhs=wb[:, j * 4 + kk, csl], start=(kk == 0), stop=(kk == 3)),
                            reads=[yT_, wb], writes=[p])
                    gsl = slice(j * D + c * 512, j * D + (c + 1) * 512)
                    if j == 0:
                        ph.op('dve', lambda e, p=p, a_=a_, m_=m_, csl=csl, gsl=gsl: e.tensor_tensor(
                            out=a_[:, csl], in0=p[:], in1=m_[:, gsl], op=ALU.mult), reads=[p, m_], writes=[a_])
                    else:
                        t_ = tmp[npj % 2]
                        ph.op('dve', lambda e, p=p, t_=t_, m_=m_, gsl=gsl: e.tensor_tensor(
                            out=t_[:], in0=p[:], in1=m_[:, gsl], op=ALU.mult), reads=[p, m_], writes=[t_])
                        ph.op('pool', lambda e, a_=a_, t_=t_, csl=csl: e.tensor_tensor(
                            out=a_[:, csl], in0=a_[:, csl], in1=t_[:], op=ALU.add), reads=[a_, t_], writes=[a_])
            ph.op('act', lambda e, a_=a_: e.copy(out=accb[:], in_=a_[:]), reads=[a_], writes=[accb])
            for k in range(8):
                ph.op('pe', lambda e, k=k: e.transpose(out=tp[0][:, k, :], in_=accb[:, k * 128:(k + 1) * 128], identity=idb[:]),
                      reads=[accb, idb], writes=[tp[0]])
            ph.op('dve', lambda e: e.tensor_copy(out=accT[:], in_=tp[0][:]), reads=[tp[0]], writes=[accT])
            for c in range(2):
                for k in range(8):
                    ph.op('pe', lambda e, c=c, k=k: e.matmul(yo[:, c * 512:(c + 1) * 512], lhsT=accT[:, k, :],
                                                             rhs=wo[:, k, c * 512:(c + 1) * 512], start=(k == 0), stop=(k == 7)),
                          reads=[accT, wo], writes=[yo])
            norm_tile(ph, [Src(lambda: yo[:, 0:512], [yo]), Src(lambda: yo[:, 512:1024], [yo])], junk, s_)
            for c in range(2):
                ph.op('dve', lambda e, o_=o_, s_=s_, ri=ri, c=c: e.scalar_tensor_tensor(
                    out=o_[:, c * 512:(c + 1) * 512], in0=yo[:, c * 512:(c + 1) * 512], scalar=s_[:, 1:2],
                    in1=Gb[ri][:, c * 512:(c + 1) * 512], op0=ALU.mult, op1=ALU.mult),
                    reads=[yo, s_, Gb[ri]], writes=[o_])
            ph.op('pool', lambda e, o_=o_, x_=x_: e.tensor_tensor(out=o_[:], in0=o_[:], in1=x_[:], op=ALU.add),
                  reads=[o_, x_], writes=[o_])
            ph.dma('sp', lambda e, o_=o_, b=b, tsl=tsl: e.dma_start(out=XA[b, tsl, :], in_=o_[:]), reads=[o_], side=o_)
            n += 1
    return ph.emit()


def phase_ffn_up(nc, l, hT, W1, AT):
    ph = Phase(nc, 'fu_%d' % l)
    w1 = ph.sb('w1', [128, 8, DFF], BF16)
    for k in range(8):
        ph.dma('pool', lambda e, k=k: e.dma_start(out=w1[:, k, :].rearrange('p (a c) -> p a c', a=2),
                                                  in_=W1[l, :, k, :].rearrange('p (a c) -> p a c', a=2)),
               writes=[w1], side=w1)
    pp = [ph.ps('pp%d' % i, [128, 512]) for i in range(4)]
    rr = [ph.sb('rr%d' % i, [128, 512], F32) for i in range(3)]
    oo = [ph.sb('oo%d' % i, [128, 512], BF16) for i in range(4)]
    n = 0
    for b in range(NB):
        for (t0, nn) in GROUPS:
            for f in range(32):
                p = pp[n % 4]
                r = rr[n % 3]
                o = oo[n % 4]
                for k in range(8):
                    ph.op('pe', lambda e, p=p, k=k, f=f, b=b, t0=t0, nn=nn: e.matmul(
                        p[:, 0:nn], lhsT=w1[:, k, f * 128:(f + 1) * 128], rhs=hT[:, b, k, t0:t0 + nn],
                        start=(k == 0), stop=(k == 7)), reads=[w1, hT], writes=[p])
                ph.op('act', lambda e, p=p, r=r, nn=nn: e.activation(out=r[:, 0:nn], in_=p[:, 0:nn], func=AF.Relu),
                      reads=[p], writes=[r])
                eng = 'dve' if n % 2 == 0 else 'pool'
                ph.op(eng, lambda e, r=r, o=o, nn=nn: e.tensor_tensor(out=o[:, 0:nn], in0=r[:, 0:nn], in1=r[:, 0:nn], op=ALU.mult),
                      reads=[r], writes=[o])
                ph.dma('sp', lambda e, o=o, b=b, f=f, t0=t0, nn=nn: e.dma_start(
                    out=AT[b, f * 128:(f + 1) * 128, t0:t0 + nn], in_=o[:, 0:nn]), reads=[o], side=o)
                n += 1
    return ph.emit()


def phase_ffn_down(nc, l, XA, AT, W2, MODV, XB, out_d, both=False, last=False):
    ph = Phase(nc, 'fd_%d' % l)
    make_eps(ph)
    w2 = ph.sb('w2', [128, 32, D], BF16)
    for k in range(32):
        ph.dma('pool', lambda e, k=k: e.dma_start(out=w2[:, k, :], in_=W2[l, :, k, :]), writes=[w2], side=w2)
    Gb = [ph.sb('Gb%d' % i, [128, D], F32) for i in range(2)]
    bcast_load(ph, 'sp', Gb[0], MODV, 2, 5)
    at = [ph.sb('at%d' % i, [128, 32, 512], BF16) for i in range(2)]
    xt = [ph.sb('xt%d' % i, [128, D], F32) for i in range(2)]
    xo = [ph.sb('xo%d' % i, [128, D], F32) for i in range(2)]
    junk = ph.sb('junk', [128, D], F32)
    ss = [ph.sb('ss%d' % i, [128, 4], F32) for i in range(2)]
    fo = [ph.ps('fo%d' % i, [128, D]) for i in range(2)]
    n = 0
    ng = 0
    for b in range(NB):
        bcast_load(ph, 'sp', Gb[1], MODV, b, 5)
        for (t0, nn) in GROUPS:
            a_ = at[ng % 2]
            ng += 1
            for k0 in range(0, 32, 4):
                ph.dma('sp', lambda e, a_=a_, b=b, t0=t0, nn=nn, k0=k0: e.dma_start(
                    out=a_[:, k0:k0 + 4, 0:nn],
                    in_=AT[b, k0 * 128:(k0 + 4) * 128, t0:t0 + nn].rearrange('(k p) t -> p k t', p=128)), writes=[a_], side=a_)
            for ti in range(nn // 128):
                i = t0 // 128 + ti
                ri = 0 if i < 2 else 1
                tsl = slice(i * 128, (i + 1) * 128)
                x_ = xt[n % 2]
                o_ = xo[n % 2]
                s_ = ss[n % 2]
                f_ = fo[n % 2]
                ph.dma('sp', lambda e, x_=x_, b=b, tsl=tsl: e.dma_start(out=x_[:], in_=XA[b, tsl, :]), writes=[x_], side=x_)
                for c in range(2):
                    for k in range(32):
                        ph.op('pe', lambda e, f_=f_, a_=a_, c=c, k=k, ti=ti: e.matmul(
                            f_[:, c * 512:(c + 1) * 512], lhsT=a_[:, k, ti * 128:(ti + 1) * 128],
                            rhs=w2[:, k, c * 512:(c + 1) * 512], start=(k == 0), stop=(k == 31)),
                            reads=[a_, w2], writes=[f_])
                norm_tile(ph, [Src(lambda f_=f_: f_[:, 0:512], [f_]), Src(lambda f_=f_: f_[:, 512:1024], [f_])], junk, s_)
                for c in range(2):
                    ph.op('dve', lambda e, o_=o_, s_=s_, f_=f_, ri=ri, c=c: e.scalar_tensor_tensor(
                        out=o_[:, c * 512:(c + 1) * 512], in0=f_[:, c * 512:(c + 1) * 512], scalar=s_[:, 1:2],
                        in1=Gb[ri][:, c * 512:(c + 1) * 512], op0=ALU.mult, op1=ALU.mult),
                        reads=[f_, s_, Gb[ri]], writes=[o_])
                ph.op('pool', lambda e, o_=o_, x_=x_: e.tensor_tensor(out=o_[:], in0=o_[:], in1=x_[:], op=ALU.add),
                      reads=[o_, x_], writes=[o_])
                if (both or last) and i >= 2:
                    ph.dma('sp', lambda e, o_=o_, b=b, i=i: e.dma_start(
                        out=out_d[b, (i - 2) * 128:(i - 1) * 128, :], in_=o_[:]), reads=[o_], side=o_)
                if both or not last:
                    ph.dma('sp', lambda e, o_=o_, b=b, tsl=tsl: e.dma_start(out=XB[b, tsl, :], in_=o_[:]), reads=[o_], side=o_)
                n += 1
    return ph.emit()


def phase_attn_a(nc, l, Z, a_sink_d, masks_d, Y):
    ph = Phase(nc, 'aa_%d' % l)
    AQT, AKT, AV = Z['AQT'][0], Z['AKT'][0], Z['AV'][0]
    QT = ph.sb('QT', [64, 8, T], BF16)
    KT = ph.sb('KT', [64, 2, T], BF16)
    V = ph.sb('V', [128, NT, 2, 65], BF16)
    es = ph.sb('es', [128, 8], F32)
    mk = ph.sb('mk', [128, 2, 512], BF16)
    ph.dma('sp', lambda e: e.dma_start(out=es[:], in_=a_sink_d[l:l + 1, :].partition_broadcast(128)), writes=[es], side=es)
    ph.op('act', lambda e: e.activation(out=es[:], in_=es[:], func=AF.Exp), reads=[es], writes=[es])
    ph.dma('pool', lambda e: e.dma_start(out=mk[:], in_=masks_d), writes=[mk], side=mk)
    pss = [ph.ps('pss%d' % i, [128, 512]) for i in range(3)]
    pos = [ph.ps('pos%d' % i, [128, 4, 65]) for i in range(2)]
    pT = [[ph.sb('pT%d_%d' % (s, c), [128, 512], BF16) for c in range(5)] for s in range(2)]
    den = [ph.sb('den%d' % i, [128, 4], F32) for i in range(2)]
    yo = [ph.sb('yo%d' % i, [128, 512], BF16) for i in range(2)]
    ns = 0
    nq = 0
    for b in range(NB):
        ph.op('pool', lambda e: e.memset(V[:], 1.0), writes=[V])
        ph.dma('sp', lambda e, b=b: e.dma_start(out=QT[:], in_=AQT[b].rearrange('(h d) t -> d h t', d=64)), writes=[QT], side=QT)
        ph.dma('sp', lambda e, b=b: e.dma_start(out=KT[:], in_=AKT[b].rearrange('(h d) t -> d h t', d=64)), writes=[KT], side=KT)
        for k in range(2):
            for (i0, i1) in TCHUNKS:
                ph.dma('sp', lambda e, b=b, k=k, i0=i0, i1=i1: e.dma_start(
                    out=V[:, i0:i1, k, 0:64],
                    in_=AV[b, i0 * 128:i1 * 128, k * 64:(k + 1) * 64].rearrange('(i p) d -> p i d', p=128)),
                    writes=[V], side=V)
        for iq in range(NT):
            if iq < 2:
                chunks = [(0, None), (1, None)]
            else:
                chunks = []
                if iq - 1 >= 2:
                    chunks.append((iq - 1, 0))
                chunks.append((iq, None))
                if iq + 1 < NT:
                    chunks.append((iq + 1, 1))
                chunks += [(0, None), (1, None)]
            y_ = yo[iq % 2]
            for kvh in range(2):
                pset = pT[nq % 2]
                po = pos[nq % 2]
                d_ = den[nq % 2]
                nq += 1
                for ci, (kt, mi) in enumerate(chunks):
                    p = pss[ns % 3]
                    ns += 1
                    pt_ = pset[ci]
                    ph.op('pe', lambda e, p=p, kvh=kvh, kt=kt, iq=iq: e.matmul(
                        p[:].rearrange('p (g q) -> p g q', g=4), lhsT=KT[:, kvh, kt * 128:(kt + 1) * 128],
                        rhs=QT[:, kvh * 4:(kvh + 1) * 4, iq * 128:(iq + 1) * 128], start=True, stop=True),
                        reads=[KT, QT], writes=[p])
                    ph.op('act', lambda e, p=p, pt_=pt_: e.activation(out=pt_[:], in_=p[:], func=AF.Exp, scale=0.125),
                          reads=[p], writes=[pt_])
                    if mi is not None:
                        ph.op('pool', lambda e, pt_=pt_, mi=mi: e.tensor_tensor(out=pt_[:], in0=pt_[:], in1=mk[:, mi, :], op=ALU.mult),
                              reads=[pt_, mk], writes=[pt_])
                for g in range(4):
                    for ci, (kt, mi) in enumerate(chunks):
                        pt_ = pset[ci]
                        ph.op('pe', lambda e, po=po, g=g, pt_=pt_, kt=kt, kvh=kvh, ci=ci, nch=len(chunks): e.matmul(
                            po[:, g, :], lhsT=pt_[:, g * 128:(g + 1) * 128], rhs=V[:, kt, kvh, :],
                            start=(ci == 0), stop=(ci == nch - 1)), reads=[pt_, V], writes=[po])
                ph.op('dve', lambda e, po=po, d_=d_, kvh=kvh: e.tensor_tensor(
                    out=d_[:], in0=po[:, :, 64], in1=es[:, kvh * 4:(kvh + 1) * 4], op=ALU.add), reads=[po, es], writes=[d_])
                ph.op('dve', lambda e, d_=d_: e.reciprocal(out=d_[:], in_=d_[:]), reads=[d_], writes=[d_])
                ph.op('dve', lambda e, po=po, d_=d_, y_=y_, kvh=kvh: e.tensor_tensor(
                    out=y_[:, kvh * 256:(kvh + 1) * 256].rearrange('p (g d) -> p g d', g=4), in0=po[:, :, 0:64],
                    in1=d_[:].unsqueeze(2).to_broadcast([128, 4, 64]), op=ALU.mult), reads=[po, d_], writes=[y_])
            ph.dma('sp', lambda e, y_=y_, b=b, iq=iq: e.dma_start(out=Y[b, iq * 128:(iq + 1) * 128, 0, :], in_=y_[:]),
                   reads=[y_], side=y_)
    return ph.emit()


B_CASES = [(5, tk) for tk in range(3, 8)] + [(0, tk) for tk in range(4)] + [(1, tk) for tk in range(4)] + \
          [(14, tk) for tk in range(12, 16)] + [(15, tk) for tk in range(12, 16)]


def host_bias_b(rel_bias_l):
    j = np.arange(128)
    out = np.empty((len(B_CASES), 128, 8, 128), np.float32)
    for ci, (R, Tk) in enumerate(B_CASES):
        kr = 2 * Tk + j // 64
        kcol = j % 64
        r = 2 * R + j // 64
        qcol = j % 64
        r0 = np.clip(r - 4, 0, 24)
        cstart = np.clip(qcol - 8, 0, 48)
        valid = (kr[:, None] >= r0[None, :]) & (kr[:, None] < r0[None, :] + 8) & \
                (kcol[:, None] >= cstart[None, :]) & (kcol[:, None] < cstart[None, :] + 16)
        dr = np.clip(kr[:, None] - r[None, :] + 7, 0, 14)
        dc = np.clip(kcol[:, None] - qcol[None, :], -15, 15) + 15
        for h in range(8):
            bt = rel_bias_l[h][dr, dc]
            out[ci, :, h, :] = np.where(valid, bt, np.float32(NEG))
    return out


def phase_attn_b(nc, l, Z, BIASB, Y):
    ph = Phase(nc, 'ab_%d' % l)
    BQT, BKT, BV = Z['BQT'][0], Z['BKT'][0], Z['BV'][0]
    QT = ph.sb('QT', [64, 8, T], BF16)
    KT = ph.sb('KT', [64, 8, T], BF16)
    V = ph.sb('V', [128, NT, 8, 65], BF16)
    bI = ph.sb('bI', [128, 5, 8, 128], F32)
    bB = ph.sb('bB', [128, 4, 8, 128], F32)
    ph.dma('sp', lambda e: e.dma_start(out=bI[:], in_=BIASB[l, 0:5].rearrange('c p h q -> p c h q')), writes=[bI], side=bI)
    pss = [ph.ps('pss%d' % i, [128, 7, 128]) for i in range(2)]
    pos = [ph.ps('pos%d' % i, [128, 4, 65]) for i in range(3)]
    sl = [ph.sb('sl%d' % i, [128, 5, 128], F32) for i in range(2)]
    pT = [ph.sb('pT%d' % i, [128, 7, 128], BF16) for i in range(2)]
    den = [ph.sb('den%d' % i, [128, 4], F32) for i in range(2)]
    yo = [ph.sb('yo%d' % i, [128, 512], BF16) for i in range(2)]
    nh = 0
    npo = 0
    for b in range(NB):
        ph.op('pool', lambda e: e.memset(V[:], 1.0), writes=[V])
        ph.dma('sp', lambda e, b=b: e.dma_start(out=QT[:], in_=BQT[b].rearrange('(h d) t -> d h t', d=64)), writes=[QT], side=QT)
        ph.dma('sp', lambda e, b=b: e.dma_start(out=KT[:], in_=BKT[b].rearrange('(h d) t -> d h t', d=64)), writes=[KT], side=KT)
        for k in range(8):
            for (i0, i1) in TCHUNKS:
                ph.dma('sp', lambda e, b=b, k=k, i0=i0, i1=i1: e.dma_start(
                    out=V[:, i0:i1, k, 0:64],
                    in_=BV[b, i0 * 128:i1 * 128, k * 64:(k + 1) * 64].rearrange('(i p) d -> p i d', p=128)),
                    writes=[V], side=V)
        for iq in range(NT):
            if iq < 2:
                loc = []
                bt = None
            else:
                R = iq - 2
                if 2 <= R <= 13:
                    loc = [(R + d + 2, d + 2) for d in range(-2, 3)]
                    bt = bI
                else:
                    base = {0: 5, 1: 9, 14: 13, 15: 17}[R]
                    tk0 = 0 if R < 2 else 12
                    loc = [(tk0 + c + 2, c) for c in range(4)]
                    bt = bB
                    ph.dma('sp', lambda e, base=base: e.dma_start(
                        out=bB[:], in_=BIASB[l, base:base + 4].rearrange('c p h q -> p c h q')), writes=[bB], side=bB)
            nl = len(loc)
            chunks = [kt for kt, _ in loc] + [0, 1]
            y_ = yo[iq % 2]
            for hh in range(2):
                po = pos[npo % 3]
                d_ = den[npo % 2]
                npo += 1
                for h4 in range(4):
                    h = hh * 4 + h4
                    p = pss[nh % 2]
                    s_ = sl[nh % 2]
                    pt_ = pT[nh % 2]
                    nh += 1
                    for ci, kt in enumerate(chunks):
                        ph.op('pe', lambda e, p=p, ci=ci, kt=kt, h=h, iq=iq: e.matmul(
                            p[:, ci, :], lhsT=KT[:, h, kt * 128:(kt + 1) * 128], rhs=QT[:, h, iq * 128:(iq + 1) * 128],
                            start=True, stop=True), reads=[KT, QT], writes=[p])
                    if nl:
                        c0 = loc[0][1]
                        for (a0, a1) in ((0, min(nl, 4)), (4, nl)):
                            if a1 <= a0:
                                continue
                            ph.op('dve', lambda e, p=p, s_=s_, a0=a0, a1=a1, bt=bt, c0=c0, h=h: e.scalar_tensor_tensor(
                                out=s_[:, a0:a1, :], in0=p[:, a0:a1, :], scalar=0.125, in1=bt[:, c0 + a0:c0 + a1, h, :],
                                op0=ALU.mult, op1=ALU.add), reads=[p, bt], writes=[s_])
                        ph.op('act', lambda e, s_=s_, pt_=pt_, nl=nl: e.activation(out=pt_[:, 0:nl, :], in_=s_[:, 0:nl, :], func=AF.Exp),
                              reads=[s_], writes=[pt_])
                    for (a0, a1) in ((nl, min(nl + 2, 4)), (max(nl, 4), nl + 2)):
                        if a1 <= a0:
                            continue
                        ph.op('act', lambda e, p=p, pt_=pt_, a0=a0, a1=a1: e.activation(
                            out=pt_[:, a0:a1, :], in_=p[:, a0:a1, :], func=AF.Exp, scale=0.125), reads=[p], writes=[pt_])
                    for ci, kt in enumerate(chunks):
                        ph.op('pe', lambda e, po=po, h4=h4, pt_=pt_, ci=ci, kt=kt, h=h, nch=len(chunks): e.matmul(
                            po[:, h4, :], lhsT=pt_[:, ci, :], rhs=V[:, kt, h, :], start=(ci == 0), stop=(ci == nch - 1)),
                            reads=[pt_, V], writes=[po])
                ph.op('dve', lambda e, po=po, d_=d_: e.reciprocal(out=d_[:], in_=po[:, :, 64]), reads=[po], writes=[d_])
                ph.op('dve', lambda e, po=po, d_=d_, y_=y_, hh=hh: e.tensor_tensor(
                    out=y_[:, hh * 256:(hh + 1) * 256].rearrange('p (g d) -> p g d', g=4), in0=po[:, :, 0:64],
                    in1=d_[:].unsqueeze(2).to_broadcast([128, 4, 64]), op=ALU.mult), reads=[po, d_], writes=[y_])
            ph.dma('sp', lambda e, y_=y_, b=b, iq=iq: e.dma_start(out=Y[b, iq * 128:(iq + 1) * 128, 1, :], in_=y_[:]),
                   reads=[y_], side=y_)
    return ph.emit()


LAYERED = ('w_mod', 'bmod3', 'ng3', 'wfm', 'wtm', 'a_sink', 'biasB', 'WB', 'WO', 'W1', 'W2', 'lsel', 'cnorm', 'dnorm',
           'dup', 'dbiasT')
FUSED = False
SCAN_DBG = {}
SKIP = set()


def kernel(**inputs):
    inputs = {k: np.asarray(v) for k, v in inputs.items()}
    shared = host_shared(inputs)
    per_core = [host_prep(inputs, c) for c in range(NCORES)]
    if FUSED:
        nc, _ = build(layers=(0, 1, 2, 3), generic=False)
        in_maps = [dict(shared, **per_core[c]) for c in range(NCORES)]
        res = run_bass_kernel_spmd(nc, in_maps, core_ids=list(range(NCORES)))
        outs = [np.asarray(r['out']) for r in res.results]
    else:
        xs = [per_core[c]['xs'] for c in range(NCORES)]
        outs = None
        for l in range(DEPTH):
            nc, _ = build(layers=(l,), generic=True)
            sh_l = {k: (np.ascontiguousarray(v[l:l + 1]) if k in LAYERED else v) for k, v in shared.items()}
            in_maps = [dict(sh_l, xs=xs[c], cT=per_core[c]['cT']) for c in range(NCORES)]
            res = run_bass_kernel_spmd(nc, in_maps, core_ids=list(range(NCORES)))
            xs = [np.ascontiguousarray(np.asarray(r['XB'])) for r in res.results]
            outs = [np.asarray(r['out']) for r in res.results]
    return np.concatenate(outs, axis=0).astype(np.float32)
```
